# Optimizing a Trainium2 kernel written in Bass

```python
import jax
import jax.numpy as jnp
from jax import lax
import numpy as np

D_MODEL = 1024
BATCH = 8
SEQ = 2048
DEPTH = 2
DEC_BATCH = 128
DEC_SEQ = 1
PAST_LEN = 16384
PAGE_SIZE = 128

D_PLE = 256
EPS = 1e-6
CONV_W = 4
D_A = D_MODEL
A_BLOCKS = 8
A_BLK = D_A // A_BLOCKS
LRU_C = 8.0
D_B = D_MODEL
HD_B = 64
H_B = D_B // HD_B
N_B = 128
G_B = 2
CONV_DIM_B = D_B + 2 * G_B * N_B
CHUNK_B = 128
D_C = 2 * D_MODEL
H_C = 16
DK_C = 128
DV_C = D_C // H_C
HK_C = H_C * DK_C
CHUNK_C = 64
SUB_C = 16
N_AB = (DEPTH + 1) // 2
N_C = DEPTH // 2
IN_AB = 2 * D_A + D_B + CONV_DIM_B + H_B
IN_C = 2 * HK_C + 2 * D_C

kernel_name = 'hybrid_rglru_ssd_hgrn2_decode_step'

F32 = jnp.float32


def rmsnorm(x, w):
    xf = x.astype(F32)
    y = xf * lax.rsqrt(jnp.mean(xf * xf, axis=-1, keepdims=True) + EPS)
    return (y * w.astype(F32)).astype(x.dtype)


def causal_conv(x, buf, w, b):
    L = x.shape[1]
    xp = jnp.concatenate([buf.astype(x.dtype), x], axis=1).astype(F32)
    w = w.astype(F32)
    y = b.astype(F32) + xp[:, 0:L] * w[0]
    for kk in range(1, CONV_W):
        y = y + xp[:, kk:kk + L] * w[kk]
    return y, xp[:, L:].astype(x.dtype)


def rg_lru(x, h0, w_r, b_r, w_i, b_i, lam):
    Bsz, L, _ = x.shape
    xb = x.reshape(Bsz, L, A_BLOCKS, A_BLK)
    r = jax.nn.sigmoid(jnp.einsum('blkc,kcd->blkd', xb, w_r.astype(F32)).reshape(Bsz, L, D_A) + b_r.astype(F32))
    gi = jax.nn.sigmoid(jnp.einsum('blkc,kcd->blkd', xb, w_i.astype(F32)).reshape(Bsz, L, D_A) + b_i.astype(F32))
    log_a = -LRU_C * r * jax.nn.softplus(-lam.astype(F32))
    a = jnp.exp(log_a)
    u = jnp.sqrt(-jnp.expm1(2.0 * log_a)) * (gi * x)

    def combine(lhs, rhs):
        a1, b1 = lhs
        a2, b2 = rhs
        return a1 * a2, a2 * b1 + b2

    a_cum, h_part = lax.associative_scan(combine, (a, u), axis=1)
    h = h_part + a_cum * h0.astype(F32)[:, None]
    return h, h[:, -1]


def ssd_scan(x, dt, a_neg, bm, cm, s0):
    Bsz, L = x.shape[:2]
    qc = CHUNK_B if L >= CHUNK_B else L
    lp = -(-L // qc) * qc
    nc = lp // qc
    hg = H_B // G_B

    def padt(t):
        return jnp.pad(t, [(0, 0), (0, lp - L)] + [(0, 0)] * (t.ndim - 2))

    xc = padt(x).reshape(Bsz, nc, qc, G_B, hg, HD_B)
    dtc = padt(dt).reshape(Bsz, nc, qc, G_B, hg)
    bc = padt(bm).reshape(Bsz, nc, qc, G_B, N_B)
    cc = padt(cm).reshape(Bsz, nc, qc, G_B, N_B)
    acs = jnp.cumsum(dtc * a_neg.reshape(G_B, hg), axis=2)
    tri = jnp.tril(jnp.ones((qc, qc), bool))
    seg = jnp.where(tri[:, :, None, None], acs[:, :, :, None] - acs[:, :, None], -jnp.inf)
    cb = jnp.einsum('bctgn,bcsgn->bctsg', cc, bc)
    wts = cb[..., None] * jnp.exp(seg) * dtc[:, :, None]
    y_diag = jnp.einsum('bctsgh,bcsghp->bctghp', wts, xc)
    a_last = acs[:, :, -1]
    st = jnp.einsum('bcsgn,bcsgh,bcsghp->bcghpn', bc, jnp.exp(a_last[:, :, None] - acs) * dtc, xc)

    def step(s, inp):
        st_c, al = inp
        return jnp.exp(al)[..., None, None] * s + st_c, s

    s_fin, s_start = lax.scan(step, s0.reshape(Bsz, G_B, hg, HD_B, N_B),
                              (st.transpose(1, 0, 2, 3, 4, 5), a_last.transpose(1, 0, 2, 3)))
    y_off = jnp.einsum('bctgn,cbghpn,bctgh->bctghp', cc, s_start, jnp.exp(acs))
    y = (y_diag + y_off).reshape(Bsz, lp, H_B, HD_B)[:, :L]
    return y, s_fin.reshape(Bsz, H_B, HD_B, N_B)


def hgrn2_scan(q, k, v, g, s0):
    Bsz, L = q.shape[:2]
    ss = SUB_C if L >= SUB_C else L
    qc = CHUNK_C if L >= CHUNK_C else -(-L // ss) * ss
    lp = -(-L // qc) * qc
    nc = lp // qc
    ns = qc // ss

    def chunks(t):
        t = jnp.pad(t, ((0, 0), (0, lp - L), (0, 0), (0, 0)))
        return t.reshape(Bsz, nc, ns, ss, H_C, t.shape[-1]).transpose(1, 0, 4, 2, 3, 5)

    tri = jnp.tril(jnp.ones((ss, ss), bool))
    off = jnp.tril(jnp.ones((ns, ns), bool), -1)
    eye = jnp.eye(ns, dtype=F32)

    def step(S, inp):
        qs, ks, vs, gs = inp
        bcum = jnp.cumsum(gs.reshape(Bsz, H_C, qc, DK_C), axis=2).reshape(Bsz, H_C, ns, ss, DK_C)
        b0 = bcum[:, :, :, :1] - gs[:, :, :, :1]
        o_inter = jnp.einsum('bhitk,bhkv->bhitv', qs * jnp.exp(bcum), S)
        q_t = qs * jnp.exp(bcum - b0)
        e_off = jnp.where(off[:, :, None, None], b0[:, :, :, None] - bcum[:, :, None], -jnp.inf)
        a_off = jnp.einsum('bhitk,bhijsk->bhijts', q_t, ks[:, :, None] * jnp.exp(e_off))
        e_diag = jnp.where(tri[:, :, None], bcum[:, :, :, :, None] - bcum[:, :, :, None], -jnp.inf)
        a_diag = jnp.einsum('bhitk,bhisk,bhitsk->bhits', qs, ks, jnp.exp(e_diag))
        a_all = a_off + eye[:, :, None, None] * a_diag[:, :, :, None]
        o_intra = jnp.einsum('bhijts,bhjsv->bhitv', a_all, vs)
        b_last = bcum[:, :, -1, -1]
        k_dec = ks * jnp.exp(b_last[:, :, None, None] - bcum)
        S_new = jnp.exp(b_last)[..., None] * S + jnp.einsum('bhisk,bhisv->bhkv', k_dec, vs)
        return S_new, o_inter + o_intra

    S_fin, o = lax.scan(step, s0.astype(F32), (chunks(q), chunks(k), chunks(v), chunks(g)))
    o = o.transpose(1, 0, 3, 4, 2, 5).reshape(Bsz, lp, H_C, DV_C)[:, :L]
    return o, S_fin


def per_layer_embed(h, p, w_proj, w_gate):
    gate = jax.nn.sigmoid((h @ w_gate).astype(F32))
    return h + (gate * (p @ w_proj).astype(F32)).astype(h.dtype)


def layer_ab(h, p, s_ah, s_ac, s_bs, s_bc, norm_w, w_in, a_conv_w, a_conv_b, a_w_r, a_b_r, a_w_i, a_b_i,
             a_lam, b_conv_w, b_conv_b, b_dt_bias, b_a_log, b_d, b_norm_w, w_out, ple_w, ple_g):
    Bsz, L, _ = h.shape
    u = rmsnorm(h, norm_w) @ w_in
    a_x, a_gate, b_z, b_xbc, b_dt = jnp.split(
        u, [D_A, 2 * D_A, 2 * D_A + D_B, 2 * D_A + D_B + CONV_DIM_B], axis=-1)
    a_xc, new_ac = causal_conv(a_x, s_ac, a_conv_w, a_conv_b)
    a_y, new_ah = rg_lru(a_xc, s_ah, a_w_r, a_b_r, a_w_i, a_b_i, a_lam)
    a_out = a_y * jax.nn.silu(a_gate.astype(F32))
    xbc, new_bc = causal_conv(b_xbc, s_bc, b_conv_w, b_conv_b)
    xbc = jax.nn.silu(xbc)
    bx, bb, bcm = jnp.split(xbc, [D_B, D_B + G_B * N_B], axis=-1)
    dt = jax.nn.softplus(b_dt.astype(F32) + b_dt_bias.astype(F32))
    a_neg = -jnp.exp(b_a_log.astype(F32))
    bx4 = bx.reshape(Bsz, L, H_B, HD_B)
    y, new_bs = ssd_scan(bx4, dt, a_neg, bb.reshape(Bsz, L, G_B, N_B), bcm.reshape(Bsz, L, G_B, N_B),
                         s_bs.astype(F32))
    y = (y + b_d.astype(F32)[:, None] * bx4).reshape(Bsz, L, D_B) * jax.nn.silu(b_z.astype(F32))
    yg = y.reshape(Bsz, L, G_B, D_B // G_B)
    yg = yg * lax.rsqrt(jnp.mean(yg * yg, axis=-1, keepdims=True) + EPS)
    b_out = yg.reshape(Bsz, L, D_B) * b_norm_w.astype(F32)
    mix = jnp.concatenate([a_out, b_out], axis=-1).astype(h.dtype) @ w_out
    h = h + mix
    h = per_layer_embed(h, p, ple_w, ple_g)
    return h, new_ah, new_ac, new_bs, new_bc


def layer_c(h, p, s_c, lb, norm_w, w_in, c_norm_w, w_out, ple_w, ple_g):
    Bsz, L, _ = h.shape
    u = rmsnorm(h, norm_w) @ w_in
    q, fx, v, gate = jnp.split(u, [HK_C, 2 * HK_C, 2 * HK_C + D_C], axis=-1)
    fx = fx.astype(F32)
    g = jnp.logaddexp(jnp.log(lb), jnp.log1p(-lb) + jax.nn.log_sigmoid(fx))
    k = (1.0 - lb) * jax.nn.sigmoid(-fx)
    shp = (Bsz, L, H_C, DK_C)
    o, new_c = hgrn2_scan(q.astype(F32).reshape(shp) * (DK_C ** -0.5), k.reshape(shp),
                          v.astype(F32).reshape(Bsz, L, H_C, DV_C), g.reshape(shp), s_c)
    o = o * lax.rsqrt(jnp.mean(o * o, axis=-1, keepdims=True) + EPS)
    o = o.reshape(Bsz, L, D_C) * c_norm_w.astype(F32) * jax.nn.silu(gate.astype(F32))
    h = h + o.astype(h.dtype) @ w_out
    h = per_layer_embed(h, p, ple_w, ple_g)
    return h, new_c


def setup_inputs(seed: int = 0) -> dict:
    key = jax.random.key(seed)
    ks = jax.random.split(key, 40)

    def nrm(k, shape, s):
        return s * jax.random.normal(k, shape, F32)

    a0 = jax.random.uniform(ks[15], (N_AB, D_A), F32, 0.9, 0.999)
    sa = a0 ** (1.0 / LRU_C)
    a_lam = jnp.log(sa) - jnp.log1p(-sa)
    dt0 = jnp.exp(jax.random.uniform(ks[18], (N_AB, H_B), F32, np.log(1e-3), np.log(0.1)))
    b_dt_bias = dt0 + jnp.log(-jnp.expm1(-dt0))
    b_a_log = jnp.log(jax.random.uniform(ks[19], (N_AB, H_B), F32, 1.0, 16.0))
    return {
        'x_prompt': nrm(ks[0], (BATCH, SEQ, D_MODEL), 1.0),
        'x_sample': nrm(ks[1], (DEC_BATCH, DEC_SEQ, D_MODEL), 1.0),
        'p_prompt': nrm(ks[2], (DEPTH, BATCH, SEQ, D_PLE), 1.0),
        'p_sample': nrm(ks[3], (DEPTH, DEC_BATCH, DEC_SEQ, D_PLE), 1.0),
        'state_a_h': nrm(ks[4], (N_AB, DEC_BATCH, D_A), 0.5),
        'state_a_conv': nrm(ks[5], (N_AB, DEC_BATCH, CONV_W - 1, D_A), 1.0),
        'state_b_ssm': nrm(ks[6], (N_AB, DEC_BATCH, H_B, HD_B, N_B), 0.3),
        'state_b_conv': nrm(ks[7], (N_AB, DEC_BATCH, CONV_W - 1, CONV_DIM_B), 1.0),
        'state_c': nrm(ks[8], (N_C, DEC_BATCH, H_C, DK_C, DV_C), 0.3),
        'norm_w': 1.0 + nrm(ks[9], (DEPTH, D_MODEL), 0.02),
        'norm_f': 1.0 + nrm(ks[10], (D_MODEL,), 0.02),
        'ab_w_in': nrm(ks[11], (N_AB, D_MODEL, IN_AB), D_MODEL ** -0.5),
        'a_conv_w': nrm(ks[12], (N_AB, CONV_W, D_A), 0.5),
        'a_conv_b': nrm(ks[13], (N_AB, D_A), 0.01),
        'a_w_r': nrm(ks[14], (N_AB, A_BLOCKS, A_BLK, A_BLK), A_BLK ** -0.5),
        'a_b_r': nrm(ks[16], (N_AB, D_A), 0.01),
        'a_w_i': nrm(ks[17], (N_AB, A_BLOCKS, A_BLK, A_BLK), A_BLK ** -0.5),
        'a_b_i': nrm(ks[20], (N_AB, D_A), 0.01),
        'a_lam': a_lam,
        'b_conv_w': nrm(ks[21], (N_AB, CONV_W, CONV_DIM_B), 0.5),
        'b_conv_b': nrm(ks[22], (N_AB, CONV_DIM_B), 0.01),
        'b_dt_bias': b_dt_bias,
        'b_a_log': b_a_log,
        'b_d': 1.0 + nrm(ks[23], (N_AB, H_B), 0.02),
        'b_norm_w': 1.0 + nrm(ks[24], (N_AB, D_B), 0.02),
        'ab_w_out': nrm(ks[25], (N_AB, D_A + D_B, D_MODEL), (D_A + D_B) ** -0.5),
        'c_w_in': nrm(ks[26], (N_C, D_MODEL, IN_C), D_MODEL ** -0.5),
        'c_lb': nrm(ks[27], (DEPTH, HK_C), 0.1),
        'c_norm_w': 1.0 + nrm(ks[28], (N_C, D_C), 0.02),
        'c_w_out': nrm(ks[29], (N_C, D_C, D_MODEL), D_C ** -0.5),
        'ple_proj': nrm(ks[30], (DEPTH, D_PLE, D_MODEL), D_PLE ** -0.5),
        'ple_gate': nrm(ks[31], (DEPTH, D_MODEL, D_MODEL), D_MODEL ** -0.5),
    }


def reference(x_prompt, x_sample, p_prompt, p_sample, state_a_h, state_a_conv, state_b_ssm, state_b_conv,
              state_c, norm_w, norm_f, ab_w_in, a_conv_w, a_conv_b, a_w_r, a_b_r, a_w_i, a_b_i, a_lam,
              b_conv_w, b_conv_b, b_dt_bias, b_a_log, b_d, b_norm_w, ab_w_out, c_w_in, c_lb, c_norm_w,
              c_w_out, ple_proj, ple_gate):
    lb_tab = jnp.cumsum(jax.nn.softmax(c_lb.astype(F32), axis=0), axis=0)
    lb_tab = lb_tab - lb_tab[:1]
    bp = x_prompt.shape[0]
    hp, hs = x_prompt, x_sample
    ah_p, ac_p, bs_p, bc_p, c_p = [], [], [], [], []
    ah_s, ac_s, bs_s, bc_s, c_s = [], [], [], [], []
    for i in range(DEPTH):
        j = i // 2
        if i % 2 == 0:
            w = (norm_w[i], ab_w_in[j], a_conv_w[j], a_conv_b[j], a_w_r[j], a_b_r[j], a_w_i[j], a_b_i[j],
                 a_lam[j], b_conv_w[j], b_conv_b[j], b_dt_bias[j], b_a_log[j], b_d[j], b_norm_w[j],
                 ab_w_out[j], ple_proj[i], ple_gate[i])
            hp, s1, s2, s3, s4 = layer_ab(
                hp, p_prompt[i], jnp.zeros((bp, D_A), F32), jnp.zeros((bp, CONV_W - 1, D_A), x_prompt.dtype),
                jnp.zeros((bp, H_B, HD_B, N_B), F32), jnp.zeros((bp, CONV_W - 1, CONV_DIM_B), x_prompt.dtype), *w)
            ah_p.append(s1); ac_p.append(s2); bs_p.append(s3); bc_p.append(s4)
            hs, s1, s2, s3, s4 = layer_ab(hs, p_sample[i], state_a_h[j], state_a_conv[j], state_b_ssm[j],
                                          state_b_conv[j], *w)
            ah_s.append(s1); ac_s.append(s2); bs_s.append(s3); bc_s.append(s4)
        else:
            lb = lb_tab[i]
            w = (norm_w[i], c_w_in[j], c_norm_w[j], c_w_out[j], ple_proj[i], ple_gate[i])
            hp, s1 = layer_c(hp, p_prompt[i], jnp.zeros((bp, H_C, DK_C, DV_C), F32), lb, *w)
            c_p.append(s1)
            hs, s1 = layer_c(hs, p_sample[i], state_c[j], lb, *w)
            c_s.append(s1)
    y_prompt = rmsnorm(hp, norm_f)
    y_sample = rmsnorm(hs, norm_f)
    return (y_prompt, y_sample,
            jnp.stack(ah_p), jnp.stack(ac_p), jnp.stack(bs_p), jnp.stack(bc_p), jnp.stack(c_p),
            jnp.stack(ah_s), jnp.stack(ac_s), jnp.stack(bs_s), jnp.stack(bc_s), jnp.stack(c_s))
```

```python
import numpy as np
import concourse.bass as bass
import concourse.mybir as mybir
from concourse.bass_utils import run_bass_kernel_spmd
from contextlib import ExitStack

F32, BF16 = mybir.dt.float32, mybir.dt.bfloat16
AF = mybir.ActivationFunctionType
ALU = mybir.AluOpType
AX = mybir.AxisListType

T_ALL = 2048
TH = 1024
NS = 16
NCOL = TH + NS
EPS = 1e-6
DBG_STOP = 2
DBG_PARTS = "NABSO"
import os as _os
SKEW_A = int(_os.environ.get("K_SKEW_A", "1"))
SKEW_C = int(_os.environ.get("K_SKEW_C", "1"))
SKEW_S = int(_os.environ.get("K_SKEW_S", "1"))
SKEW_H = int(_os.environ.get("K_SKEW_H", "1"))
NS_A = int(_os.environ.get("K_NS_A", "2"))
NS_S = int(_os.environ.get("K_NS_S", "2"))
NS_H = int(_os.environ.get("K_NS_H", "2"))


class Buf:
    __slots__ = ("name", "w", "r")

    def __init__(self, name):
        self.name = name
        self.w = None
        self.r = {}


class Ctx:
    R = 8

    def __init__(self, nc, es):
        self.nc = nc
        self.eng = {"pe": nc.tensor, "act": nc.scalar, "dve": nc.vector, "pool": nc.gpsimd, "sp": nc.sync}
        self.sem = {e: es.enter_context(nc.semaphore("s_" + e)) for e in ("pe", "act", "dve", "pool")}
        self.cnt = {e: 0 for e in self.sem}
        self.known = {e: {} for e in self.eng}
        self.RQ = {"sp": 8, "pool": 3}
        self.ring = {q: [es.enter_context(nc.semaphore(f"d_{q}{i}")) for i in range(self.RQ[q])] for q in ("sp", "pool")}
        self.rcnt = {q: [0] * self.RQ[q] for q in self.ring}
        self.ridx = {q: 0 for q in self.ring}
        self.nwait = 0
        self.marks = []

    def _wait(self, e, ev):
        key, h, v = ev
        if self.known[e].get(key, 0) >= v:
            return
        self.eng[e].wait_ge(h, v)
        self.known[e][key] = v
        self.nwait += 1

    def _deps(self, e, reads, writes):
        for b in reads:
            if b.w is not None:
                if not (b.w[0] == "pe" and e == "pe"):
                    self._wait(e, b.w)
        for b in writes:
            if b.w is not None:
                if not (b.w[0] == "pe" and e == "pe"):
                    self._wait(e, b.w)
            for k, ev in b.r.items():
                if ev[0] == e and e == "pe":
                    continue
                self._wait(e, ev)

    def _post(self, ev, reads, writes):
        for b in reads:
            b.r[ev[0]] = ev
        for b in writes:
            b.w = ev
            b.r = {}

    def op(self, e, fn, reads=(), writes=(), inc=True):
        self._deps(e, reads, writes)
        inst = fn()
        if inc:
            self.cnt[e] += 1
            assert self.cnt[e] < 60000
            inst.then_inc(self.sem[e], 1)
            ev = (e, self.sem[e], self.cnt[e])
        else:
            ev = (e, self.sem[e], self.cnt[e] + 1)
        self._post(ev, reads, writes)
        return inst

    def dma(self, q, out, in_, reads=(), writes=()):
        self._deps(q, reads, writes)
        i = self.ridx[q] % self.RQ[q]
        self.ridx[q] += 1
        if self.rcnt[q][i] > 0:
            self._wait(q, (f"dma_{q}{i}", self.ring[q][i], self.rcnt[q][i]))
        self.rcnt[q][i] += 16
        assert self.rcnt[q][i] < 60000
        inst = self.eng[q].dma_start(out=out, in_=in_)
        inst.then_inc(self.ring[q][i], 16)
        ev = (f"dma_{q}{i}", self.ring[q][i], self.rcnt[q][i])
        self._post(ev, reads, writes)

    def all_events(self):
        evs = [(e, self.sem[e], self.cnt[e]) for e in self.sem if self.cnt[e] > 0]
        for q in self.ring:
            for i in range(self.RQ[q]):
                if self.rcnt[q][i] > 0:
                    evs.append((f"dma_{q}{i}", self.ring[q][i], self.rcnt[q][i]))
        return evs

    def barrier(self, engines=("act", "dve", "sp"), with_pool=False):
        import inspect
        self.marks.append((inspect.stack()[1].function, dict(self.cnt)))
        evs = self.all_events()
        if not with_pool:
            evs = [ev for ev in evs if ev[0] != "pool" and not ev[0].startswith("dma_pool")]
        for e in engines:
            for ev in evs:
                if ev[0] == e:
                    continue
                self._wait(e, ev)


class T:
    def __init__(self, nc, es, name, shape, dtype=F32, nb=None):
        self.ap = es.enter_context(nc.sbuf_tensor(name, list(shape), dtype))
        if nb is None:
            self.b = Buf(name)
        else:
            self.b = [Buf(f"{name}{i}") for i in range(nb)]


def build_program():
    nc = bass.Bass("TRN2", target_bir_lowering=False)

    def din(name, shape):
        return nc.dram_tensor(name, list(shape), F32, kind="ExternalInput").ap()

    def dout(name, shape):
        return nc.dram_tensor(name, list(shape), F32, kind="ExternalOutput").ap()

    D = {}
    for name, shape in IN_SHAPES.items():
        D[name] = din(name, shape)
    O = {}
    for name, shape in OUT_SHAPES.items():
        O[name] = dout(name, shape)
    scr_bc = nc.dram_tensor("scr_bc", [2, 2048], F32, kind="Internal").ap()
    scr_bc_b = Buf("scr_bc")

    with ExitStack() as es:
        C = Ctx(nc, es)
        V, S_, G = nc.vector, nc.scalar, nc.gpsimd

        uid = {"n": 0}

        def mk(name, shape, dtype=F32, nb=None, stack=es):
            uid["n"] += 1
            return T(nc, stack, f"{name}_{uid['n']}", shape, dtype, nb)

        def act(out, in_, func, R, W, **kw):
            return C.op("act", lambda: S_.activation(out=out, in_=in_, func=func, **kw), R, W)

        def acopy(out, in_, R, W):
            return C.op("act", lambda: S_.copy(out=out, in_=in_), R, W)

        def vcopy(out, in_, R, W):
            return C.op("dve", lambda: V.tensor_copy(out=out, in_=in_), R, W)

        def tt(out, in0, in1, op, R, W):
            return C.op("dve", lambda: V.tensor_tensor(out=out, in0=in0, in1=in1, op=op), R, W)

        def ts(out, in0, s1, s2, op0, op1, R, W):
            if s2 is None:
                return C.op("dve", lambda: V.tensor_scalar(out=out, in0=in0, scalar1=s1, scalar2=None, op0=op0), R, W)
            return C.op("dve", lambda: V.tensor_scalar(out=out, in0=in0, scalar1=s1, scalar2=s2, op0=op0, op1=op1), R, W)

        def stt(out, in0, scalar, in1, op0, op1, R, W):
            return C.op("dve", lambda: V.scalar_tensor_tensor(out=out, in0=in0, scalar=scalar, in1=in1, op0=op0, op1=op1), R, W)

        def mm(out, lhsT, rhs, start, stop, R, W, inc):
            return C.op("pe", lambda: nc.tensor.matmul(out, lhsT=lhsT, rhs=rhs, start=start, stop=stop), R, W, inc=True)

        def tp(out, in_, ident, R, W, inc):
            return C.op("pe", lambda: nc.tensor.transpose(out, in_, ident), R, W, inc=True)

        banks = [es.enter_context(nc.psum_tensor(f"pb{i}", [128, 512], F32)) for i in range(8)]
        bbuf = [Buf(f"pb{i}") for i in range(8)]
        bstate = {"i": 0}

        def bank():
            i = bstate["i"] % 8
            bstate["i"] += 1
            return banks[i], bbuf[i]

        NW = 5
        wpool = [mk(f"wb{i}", [128, 4096], BF16) for i in range(NW)]
        wstate = {"i": 0}

        def wload(Wap, kc, col0, ncols):
            t = wpool[wstate["i"] % NW]
            wstate["i"] += 1
            view = t.ap[:, 0:kc * ncols].rearrange("p (k f) -> p k f", k=kc)
            src = Wap.rearrange("(k p) f -> p k f", p=128)[:, :, col0:col0 + ncols]
            C.dma("pool", view, src, reads=[], writes=[t.b])
            return view, t.b

        identF = mk("identF", [128, 128])
        identB = mk("identB", [128, 128], BF16)
        mle = mk("mle", [128, 128])
        mgt = mk("mgt", [128, 128])
        onesF = mk("onesF", [128, 128])
        cbuf = Buf("consts")

        C.op("pool", lambda: G.memset(onesF.ap[:], 1.0), [], [cbuf])
        C.op("pool", lambda: G.affine_select(out=identF.ap[:], in_=onesF.ap[:], pattern=[[-1, 128]], compare_op=ALU.is_equal,
                                             fill=0.0, base=0, channel_multiplier=1), [cbuf], [identF.b])
        C.op("pool", lambda: G.affine_select(out=mle.ap[:], in_=onesF.ap[:], pattern=[[1, 128]], compare_op=ALU.is_ge,
                                             fill=0.0, base=0, channel_multiplier=-1), [cbuf], [mle.b])
        C.op("pool", lambda: G.affine_select(out=mgt.ap[:], in_=onesF.ap[:], pattern=[[-1, 128]], compare_op=ALU.is_gt,
                                             fill=0.0, base=0, channel_multiplier=1), [cbuf], [mgt.b])
        vcopy(identB.ap[:], identF.ap[:], [identF.b], [identB.b])
        onesB = mk("onesB", [128, 128], BF16)
        mgtB = mk("mgtB", [128, 128], BF16)
        vcopy(onesB.ap[:], onesF.ap[:], [cbuf], [onesB.b])
        vcopy(mgtB.ap[:], mgt.ap[:], [mgt.b], [mgtB.b])

        vt = mk("vt", [128, 256])
        VCOL = {}
        with ExitStack() as es0:
            st1 = mk("vst1", [128, 128], stack=es0)
            st2 = mk("vst2", [128, 128], stack=es0)
            C.op("dve", lambda: V.memset(st1.ap[:], 0.0), [], [st1.b])
            C.op("dve", lambda: V.memset(st2.ap[:], 0.0), [], [st2.b])
            rows1 = [("norm_w", D["norm_w"].rearrange("l (c p) -> (l c) p", p=128), 16),
                     ("norm_f", D["norm_f"].rearrange("(c p) -> c p", p=128), 8),
                     ("a_conv_w", D["a_conv_w"].rearrange("k (c p) -> (k c) p", p=128), 32),
                     ("a_conv_b", D["a_conv_b"].rearrange("(c p) -> c p", p=128), 8),
                     ("a_b_r", D["a_b_r"].rearrange("(c p) -> c p", p=128), 8),
                     ("a_b_i", D["a_b_i"].rearrange("(c p) -> c p", p=128), 8),
                     ("a_lam", D["a_lam"].rearrange("(c p) -> c p", p=128), 8),
                     ("b_norm_w", D["b_norm_w"].rearrange("(c p) -> c p", p=128), 8),
                     ("c_norm_w", D["c_norm_w"].rearrange("(c p) -> c p", p=128), 16)]
            rows2 = [("b_conv_w", D["b_conv_w"].rearrange("k (c p) -> (k c) p", p=128), 48),
                     ("b_conv_b", D["b_conv_b"].rearrange("(c p) -> c p", p=128), 12),
                     ("c_lb", D["c_lb"].rearrange("l (c p) -> (l c) p", p=128), 32)]
            for st, rows, base in ((st1, rows1, 0), (st2, rows2, 128)):
                r0 = 0
                for nm, ap, n in rows:
                    VCOL[nm] = base + r0
                    C.dma("sp", st.ap[r0:r0 + n, :], ap, reads=[], writes=[st.b])
                    r0 += n
                pb, pbb = bank()
                tp(pb[:, 0:128], st.ap[:, :], identF.ap[:], [st.b, identF.b], [pbb], True)
                vcopy(vt.ap[:, base:base + 128], pb[:, 0:128], [pbb], [vt.b])
            C.barrier(engines=("pe", "act", "dve", "pool", "sp"), with_pool=True)

        def vcol(nm, i):
            j = VCOL[nm] + i
            return vt.ap[:, j:j + 1]

        dv = mk("dv", [128, 64])
        la = VCOL["a_lam"]
        act(dv.ap[:, 0:8], vt.ap[:, la:la + 8], AF.Exp, [vt.b], [dv.b], scale=-1.0)
        act(dv.ap[:, 0:8], dv.ap[:, 0:8], AF.Ln, [dv.b], [dv.b], bias=1.0, scale=1.0)
        ts(dv.ap[:, 8:16], dv.ap[:, 0:8], -16.0, None, ALU.mult, None, [dv.b], [dv.b])
        ts(dv.ap[:, 0:8], dv.ap[:, 0:8], -8.0, None, ALU.mult, None, [dv.b], [dv.b])
        cl = VCOL["c_lb"]
        tt(dv.ap[:, 16:32], vt.ap[:, cl + 16:cl + 32], vt.ap[:, cl:cl + 16], ALU.subtract, [vt.b, dv.b], [dv.b])
        act(dv.ap[:, 16:32], dv.ap[:, 16:32], AF.Sigmoid, [dv.b], [dv.b])
        ts(dv.ap[:, 32:48], dv.ap[:, 16:32], -1.0, 1.0, ALU.mult, ALU.add, [dv.b], [dv.b])
        ts(dv.ap[:, 48:64], dv.ap[:, 32:48], -1.0, None, ALU.mult, None, [dv.b], [dv.b])

        hb = mk("hb", [128, 64])
        C.dma("sp", hb.ap[:, 0:16], D["b_dt_bias"].rearrange("(o h) -> o h", o=1).partition_broadcast(128), [], [hb.b])
        C.dma("sp", hb.ap[:, 16:32], D["b_a_log"].rearrange("(o h) -> o h", o=1).partition_broadcast(128), [], [hb.b])
        C.dma("sp", hb.ap[:, 32:48], D["b_d"].rearrange("(o h) -> o h", o=1).partition_broadcast(128), [], [hb.b])
        act(hb.ap[:, 16:32], hb.ap[:, 16:32], AF.Exp, [hb.b], [hb.b])
        ts(hb.ap[:, 16:32], hb.ap[:, 16:32], -1.0, None, ALU.mult, None, [hb.b], [hb.b])

        wdt = mk("wdt", [128, 8, 16], BF16)
        C.dma("pool", wdt.ap[:], D["ab_w_in"].rearrange("(k p) f -> p k f", p=128)[:, :, 4608:4624], [], [wdt.b])

        hT = mk("hT", [128, 8, NCOL], F32, nb=8)
        xnT = mk("xnT", [128, 8, NCOL], BF16, nb=8)
        hTb = xnT
        tailA = mk("tailA", [128, 8, 3], F32, nb=8)
        tailB = mk("tailB", [128, 12, 3], F32, nb=12)
        hlast = mk("hlast", [128, 8], F32)
        Sssm = mk("Sssm", [128, 1024], F32, nb=None)
        Sssm_b = [Buf("Sssm0"), Buf("Sssm1")]
        Sc = mk("Sc", [128, 16, 128], F32, nb=16)
        mixT = mk("mixT", [128, 8, NCOL], BF16, nb=8)
        C.op("dve", lambda: V.memset(Sssm.ap[:], 0.0), [], Sssm_b)
        C.op("dve", lambda: V.memset(Sc.ap[:], 0.0), [], Sc.b)

        def run_streams(gens, skew, max_active=2, filler=None, filler_rate=2):
            pending = list(gens)
            active = []
            steps = {}
            while pending or active or filler is not None:
                if pending and (not active or (len(active) < max_active and steps[id(active[-1])] >= skew)):
                    g = pending.pop(0)
                    active.append(g)
                    steps[id(g)] = 0
                for g in list(active):
                    try:
                        next(g)
                        steps[id(g)] += 1
                    except StopIteration:
                        assert filler is None, "filler stream must finish before any stream ends"
                        active.remove(g)
                if filler is not None:
                    for _ in range(filler_rate):
                        try:
                            next(filler)
                        except StopIteration:
                            filler = None
                            break

        def tiles_of(ncols):
            ts_ = [(0, 512), (512, 512)]
            if ncols > TH:
                ts_.append((TH, ncols - TH))
            return ts_

        def rmsnorm(ncols, wname, wbase, out_t, out_is_f32=False):
            tls = tiles_of(ncols)
            pbs = [bank() for _ in tls]
            with ExitStack() as esl:
                sq = [mk(f"sq{i}", [128, NCOL], BF16, stack=esl) for i in range(2)]
                rstd = mk("rstd", [128, NCOL], stack=esl)
                for c in range(8):
                    s = sq[c % 2]
                    act(s.ap[:, 0:ncols], hT.ap[:, c, 0:ncols], AF.Square, [hT.b[c]], [s.b])
                    for (t0, tn), (pb, pbb) in zip(tls, pbs):
                        mm(pb[:, 0:tn], onesB.ap[:], s.ap[:, t0:t0 + tn], c == 0, c == 7, [s.b, onesB.b], [pbb], c == 7)
                for (t0, tn), (pb, pbb) in zip(tls, pbs):
                    act(rstd.ap[:, t0:t0 + tn], pb[:, 0:tn], AF.Ln, [pbb], [rstd.b], scale=1.0 / 1024.0, bias=EPS)
                act(rstd.ap[:, 0:ncols], rstd.ap[:, 0:ncols], AF.Exp, [rstd.b], [rstd.b], scale=-0.5)
                for c in range(8):
                    stt(out_t.ap[:, c, 0:ncols], hT.ap[:, c, 0:ncols], vcol(wname, wbase + c), rstd.ap[:, 0:ncols],
                        ALU.mult, ALU.mult, [hT.b[c], rstd.b, vt.b], [out_t.b[c]])
                C.barrier()

        def proj_fm(wv, wb, fcol, src, src_b, ncols, evac, kcs=8):
            for (t0, tn) in tiles_of(ncols):
                pb, pbb = bank()
                for kc in range(kcs):
                    mm(pb[:, 0:tn], wv[:, kc, fcol:fcol + 128], src[:, kc, t0:t0 + tn], kc == 0, kc == kcs - 1,
                       [wb, src_b[kc]], [pbb], kc == kcs - 1)
                evac(pb, pbb, t0, tn)

        def out_proj_gen(Wrows, ncols):
            blocks = [wload(Wrows, 8, 0, 512), wload(Wrows, 8, 512, 512)]
            for c in range(8):
                wv, wb = blocks[c // 4]
                for (t0, tn) in tiles_of(ncols):
                    pb, pbb = bank()
                    for kc in range(8):
                        mm(pb[:, 0:tn], wv[:, kc, (c % 4) * 128:(c % 4 + 1) * 128], mixT.ap[:, kc, t0:t0 + tn], kc == 0, kc == 7,
                           [wb, mixT.b[kc]], [pbb], True)
                    tt(hT.ap[:, c, t0:t0 + tn], hT.ap[:, c, t0:t0 + tn], pb[:, 0:tn], ALU.add, [pbb, hT.b[c]], [hT.b[c]])
                    yield

        def out_proj(Wrows, ncols):
            blocks = [wload(Wrows, 8, 0, 512), wload(Wrows, 8, 512, 512)]
            for c in range(8):
                wv, wb = blocks[c // 4]

                def ev(pb, pbb, t0, tn, c=c):
                    tt(hT.ap[:, c, t0:t0 + tn], hT.ap[:, c, t0:t0 + tn], pb[:, 0:tn], ALU.add, [pbb, hT.b[c]], [hT.b[c]])
                proj_fm(wv, wb, (c % 4) * 128, mixT.ap, mixT.b, ncols, ev, kcs=8)

        def ple(layer, ps_i, ncols):
            Wg = D["ple_gate"][layer]
            Wp = D["ple_proj"][layer]
            gblk = [wload(Wg, 8, 0, 512)]
            pblk = wload(Wp, 2, 0, 1024)
            gblk.append(wload(Wg, 8, 512, 512))
            with ExitStack() as esl:
                pin = mk("pin", [128, 8, 256], stack=esl)
                psn = mk("psn", [16, 256], stack=esl)
                pTl = mk("pTl", [128, 2, NCOL], BF16, stack=esl)
                sg = [mk(f"plesg{i}", [128, 512], stack=esl) for i in range(2)]
                r0 = ps_i * TH
                C.dma("sp", pin.ap[:], D["pp"][layer, r0:r0 + TH, :].rearrange("(j p) f -> p j f", p=128), [], [pin.b])
                for kc in range(2):
                    for g in range(2):
                        pb, pbb = bank()
                        for j in range(4):
                            tp(pb[:, j * 128:(j + 1) * 128], pin.ap[:, g * 4 + j, kc * 128:(kc + 1) * 128], identF.ap[:],
                               [pin.b, identF.b], [pbb], j == 3)
                        (acopy if g == 0 else vcopy)(pTl.ap[:, kc, g * 512:(g + 1) * 512], pb[:, :], [pbb], [pTl.b])
                if ncols > TH:
                    C.dma("sp", psn.ap[:], D["psm"][layer], [], [psn.b])
                    for kc in range(2):
                        pb, pbb = bank()
                        tp(pb[:, 0:16], psn.ap[:, kc * 128:(kc + 1) * 128], identF.ap[0:16, 0:16], [psn.b, identF.b], [pbb], True)
                        vcopy(pTl.ap[:, kc, TH:TH + 16], pb[:, 0:16], [pbb], [pTl.b])
                for c in range(8):
                    (acopy if c % 2 == 0 else vcopy)(hTb.ap[:, c, 0:ncols], hT.ap[:, c, 0:ncols], [hT.b[c]], [hTb.b[c]])
                k = 0
                for c in range(8):
                    wv, wb = gblk[c // 4]
                    for (t0, tn) in tiles_of(ncols):
                        pg, pgb = bank()
                        for kc in range(8):
                            mm(pg[:, 0:tn], wv[:, kc, (c % 4) * 128:(c % 4 + 1) * 128], hTb.ap[:, kc, t0:t0 + tn], kc == 0, kc == 7,
                               [wb, hTb.b[kc]], [pgb], kc == 7)
                        pq, pqb = bank()
                        for kc in range(2):
                            mm(pq[:, 0:tn], pblk[0][:, kc, c * 128:(c + 1) * 128], pTl.ap[:, kc, t0:t0 + tn], kc == 0, kc == 1,
                               [pblk[1], pTl.b], [pqb], kc == 1)
                        s_ = sg[k % 2]
                        k += 1
                        act(s_.ap[:, 0:tn], pg[:, 0:tn], AF.Sigmoid, [pgb], [s_.b])
                        tt(s_.ap[:, 0:tn], s_.ap[:, 0:tn], pq[:, 0:tn], ALU.mult, [s_.b, pqb], [s_.b])
                        tt(hT.ap[:, c, t0:t0 + tn], hT.ap[:, c, t0:t0 + tn], s_.ap[:, 0:tn], ALU.add, [s_.b, hT.b[c]], [hT.b[c]])
                C.barrier()

        def phase0(ps_i, ncols):
            with ExitStack() as esl:
                xin = [mk(f"xin{i}", [128, 4, 1024], stack=esl) for i in range(2)]
                xsn = mk("xsn", [16, 1024], stack=esl)
                k = 0
                for g in range(2):
                    xi = xin[g]
                    r0 = ps_i * TH + g * 512
                    C.dma("sp", xi.ap[:], D["xp"][r0:r0 + 512, :].rearrange("(j p) f -> p j f", p=128), [], [xi.b])
                    for c in range(8):
                        pb, pbb = bank()
                        for j in range(4):
                            tp(pb[:, j * 128:(j + 1) * 128], xi.ap[:, j, c * 128:(c + 1) * 128], identF.ap[:], [xi.b, identF.b], [pbb], j == 3)
                        if k % 2 == 0:
                            acopy(hT.ap[:, c, g * 512:(g + 1) * 512], pb[:, :], [pbb], [hT.b[c]])
                        else:
                            vcopy(hT.ap[:, c, g * 512:(g + 1) * 512], pb[:, :], [pbb], [hT.b[c]])
                        k += 1
                if ncols > TH:
                    C.dma("sp", xsn.ap[:], D["xs"][:, :], [], [xsn.b])
                    for c in range(8):
                        pb, pbb = bank()
                        tp(pb[:, 0:16], xsn.ap[:, c * 128:(c + 1) * 128], identF.ap[0:16, 0:16], [xsn.b, identF.b], [pbb], True)
                        vcopy(hT.ap[:, c, TH:TH + 16], pb[:, 0:16], [pbb], [hT.b[c]])
                C.barrier()

        def layer0_A(ps_i, ncols):
            Win = D["ab_w_in"]
            has_s = ncols > TH
            n = ncols
            with ExitStack() as esl:
                sets = [dict(ax=mk("ax", [128, 3 + TH], BF16, stack=esl), dg=mk("dgA", [128, 4, 128], BF16, stack=esl),
                             xc=mk("xc", [128, NCOL], stack=esl), rr=mk("rr", [128, NCOL], stack=esl),
                             gi=mk("gi", [128, NCOL], stack=esl), aa=mk("aa", [128, NCOL], stack=esl),
                             hh=mk("hh", [128, NCOL], stack=esl)) for _ in range(NS_A)]
                tw = wpool[wstate["i"] % NW]
                wstate["i"] += 1
                wri = tw.ap[:, 0:2048].rearrange("p (k f) -> p k f", k=16)
                C.dma("pool", wri[:, 0:8, :], D["a_w_r"].rearrange("k c d -> c k d"), [], [tw.b])
                C.dma("pool", wri[:, 8:16, :], D["a_w_i"].rearrange("k c d -> c k d"), [], [tw.b])
                blk_x = [wload(Win, 8, 0, 512)]
                blk_g = [wload(Win, 8, 1024, 512)]
                blk_x.append(wload(Win, 8, 512, 512))
                blk_g.append(wload(Win, 8, 1536, 512))

                def chunk_gen(c, Bs):
                    ax, xc, rr, gi, aa, hh, dg = Bs["ax"], Bs["xc"], Bs["rr"], Bs["gi"], Bs["aa"], Bs["hh"], Bs["dg"]
                    xcb_ap = hh.ap[:, 0:NCOL // 2].bitcast(BF16)
                    wv, wb = blk_x[c // 4]
                    if ps_i == 0:
                        C.op("dve", lambda: V.memset(ax.ap[:, 0:3], 0.0), [], [ax.b])
                    else:
                        vcopy(ax.ap[:, 0:3], tailA.ap[:, c, :], [tailA.b[c]], [ax.b])
                    for kk in range(4):
                        ts(dg.ap[:, kk, :], identB.ap[:], vcol("a_conv_w", kk * 8 + c), None, ALU.mult, None, [identB.b, vt.b], [dg.b])

                    def ev_x(pb, pbb, t0, tn):
                        if t0 < TH:
                            vcopy(ax.ap[:, 3 + t0:3 + t0 + tn], pb[:, 0:tn], [pbb], [ax.b])
                            if t0 + tn == TH:
                                vcopy(tailA.ap[:, c, :], pb[:, tn - 3:tn], [pbb], [tailA.b[c]])
                        else:
                            vcopy(axsT.ap[:, c, :], pb[:, 0:tn], [pbb], [axsT.b])
                    proj_fm(wv, wb, (c % 4) * 128, xnT.ap, xnT.b, ncols, ev_x)
                    yield
                    for (t0, tn) in tiles_of(TH):
                        pb, pbb = bank()
                        for kk in range(4):
                            mm(pb[:, 0:tn], dg.ap[:, kk, :], ax.ap[:, t0 + kk:t0 + kk + tn], kk == 0, kk == 3, [dg.b, ax.b], [pbb], kk == 3)
                        ts(xc.ap[:, t0:t0 + tn], pb[:, 0:tn], vcol("a_conv_b", c), None, ALU.add, None, [pbb, vt.b], [xc.b])
                    yield
                    if has_s:
                        sl = slice(TH, TH + NS)
                        ts(xc.ap[:, sl], axsT.ap[:, c, :], vcol("a_conv_w", 3 * 8 + c), vcol("a_conv_b", c), ALU.mult, ALU.add,
                           [axsT.b, vt.b], [xc.b])
                        for kk in range(3):
                            stt(xc.ap[:, sl], sacT.ap[:, c, kk, :], vcol("a_conv_w", kk * 8 + c), xc.ap[:, sl], ALU.mult, ALU.add,
                                [sacT.b, xc.b, vt.b], [xc.b])
                    vcopy(xcb_ap[:, 0:ncols], xc.ap[:, 0:ncols], [xc.b], [hh.b])
                    yield
                    for (t0, tn) in tiles_of(ncols):
                        pb, pbb = bank()
                        mm(pb[:, 0:tn], wri[:, c, :], xcb_ap[:, t0:t0 + tn], True, True, [tw.b, hh.b], [pbb], True)
                        act(rr.ap[:, t0:t0 + tn], pb[:, 0:tn], AF.Sigmoid, [pbb, vt.b], [rr.b], bias=vcol("a_b_r", c), scale=1.0)
                        pb, pbb = bank()
                        mm(pb[:, 0:tn], wri[:, 8 + c, :], xcb_ap[:, t0:t0 + tn], True, True, [tw.b, hh.b], [pbb], True)
                        act(gi.ap[:, t0:t0 + tn], pb[:, 0:tn], AF.Sigmoid, [pbb, vt.b], [gi.b], bias=vcol("a_b_i", c), scale=1.0)
                    yield
                    act(aa.ap[:, 0:n], rr.ap[:, 0:n], AF.Exp, [rr.b, dv.b], [aa.b], scale=dv.ap[:, c:c + 1])
                    act(rr.ap[:, 0:n], rr.ap[:, 0:n], AF.Exp, [rr.b, dv.b], [rr.b], scale=dv.ap[:, 8 + c:9 + c])
                    yield
                    act(rr.ap[:, 0:n], rr.ap[:, 0:n], AF.Sqrt, [rr.b], [rr.b], scale=-1.0, bias=1.0)
                    tt(gi.ap[:, 0:n], gi.ap[:, 0:n], rr.ap[:, 0:n], ALU.mult, [gi.b, rr.b], [gi.b])
                    tt(gi.ap[:, 0:n], gi.ap[:, 0:n], xc.ap[:, 0:n], ALU.mult, [gi.b, xc.b], [gi.b])
                    yield
                    init = 0.0 if ps_i == 0 else hlast.ap[:, c:c + 1]
                    C.op("dve", lambda: V.tensor_tensor_scan(out=hh.ap[:, 0:TH], data0=aa.ap[:, 0:TH], data1=gi.ap[:, 0:TH], initial=init,
                                                             op0=ALU.mult, op1=ALU.add), [aa.b, gi.b, hlast.b], [hh.b])
                    vcopy(hlast.ap[:, c:c + 1], hh.ap[:, TH - 1:TH], [hh.b], [hlast.b])
                    if has_s:
                        sl = slice(TH, TH + NS)
                        tt(hh.ap[:, sl], aa.ap[:, sl], sahT.ap[:, c, :], ALU.mult, [aa.b, sahT.b], [hh.b])
                        tt(hh.ap[:, sl], hh.ap[:, sl], gi.ap[:, sl], ALU.add, [hh.b, gi.b], [hh.b])
                        vcopy(ahsT.ap[:, c, :], hh.ap[:, sl], [hh.b], [ahsT.b])
                    yield
                    wv, wb = blk_g[c // 4]

                    def ev_g(pb, pbb, t0, tn):
                        act(rr.ap[:, t0:t0 + tn], pb[:, 0:tn], AF.Silu, [pbb], [rr.b])
                        tt(mixT.ap[:, c, t0:t0 + tn], hh.ap[:, t0:t0 + tn], rr.ap[:, t0:t0 + tn], ALU.mult, [hh.b, rr.b], [mixT.b[c]])
                    proj_fm(wv, wb, (c % 4) * 128, xnT.ap, xnT.b, ncols, ev_g)
                    yield

                run_streams([chunk_gen(c, sets[c % NS_A]) for c in range(8)], skew=SKEW_A, max_active=NS_A)
                C.barrier()

        def layer0_B(ps_i, ncols):
            Win = D["ab_w_in"]
            has_s = ncols > TH
            NCH = TH // 128
            with ExitStack() as esl:
                csets = [dict(bx=mk("bx", [128, 3 + TH], BF16, stack=esl), bcs=mk("bcs", [128, NS], stack=esl),
                              dg=mk("dgB", [128, 4, 128], BF16, stack=esl)) for _ in range(2)]
                xbcT = mk("xbcT", [128, 6, NCOL], BF16, nb=6, stack=esl)
                dtt = mk("dtt", [128, NCH, 16], stack=esl)
                dta = mk("dta", [128, NCH, 16], stack=esl)
                Sb = mk("Sb", [128, 512], BF16, stack=esl)
                ssets = [dict(eall=mk("eall", [128, 3, 16], stack=esl), Rm=mk("Rm", [128, 8, 128], BF16, stack=esl),
                              Ee=mk("Ee", [128, 8, 128], BF16, stack=esl), cbm=mk("cbm", [128, 128], stack=esl),
                              Wt=mk("Wt", [128, 8, 128], BF16, stack=esl), xtm=mk("xtm", [128, 512], BF16, stack=esl),
                              btm=mk("btm", [128, 128], BF16, stack=esl), xdt=mk("xdt", [128, 512], BF16, stack=esl),
                              xD=mk("xD", [128, 512], BF16, stack=esl), xdt2=mk("xdt2", [128, 512], BF16, stack=esl),
                              ytmp=mk("ytmp", [128, 512], stack=esl), yy=mk("yy", [128, 512], stack=esl),
                              sz=mk("sz", [128, 512], stack=esl), ynb=mk("ynb", [128, 512], BF16, stack=esl),
                              ssq=mk("ssq", [128, 2], stack=esl)) for _ in range(NS_S)]

                pb, pbb = bank()
                for j in range(NCH):
                    for kc in range(8):
                        mm(pb[:, j * 16:(j + 1) * 16], xnT.ap[:, kc, j * 128:(j + 1) * 128], wdt.ap[:, kc, :], kc == 0, kc == 7,
                           [xnT.b[kc], wdt.b], [pbb], kc == 7 and j == NCH - 1)
                tt(dtt.ap[:], pb[:, 0:NCH * 16].rearrange("p (j h) -> p j h", h=16),
                   hb.ap[:, 0:16].unsqueeze(1).to_broadcast([128, NCH, 16]), ALU.add, [pbb, hb.b], [dtt.b])
                act(dtt.ap[:], dtt.ap[:], AF.Exp, [dtt.b], [dtt.b])
                act(dtt.ap[:], dtt.ap[:], AF.Ln, [dtt.b], [dtt.b], bias=1.0, scale=1.0)
                tt(dta.ap[:], dtt.ap[:], hb.ap[:, 16:32].unsqueeze(1).to_broadcast([128, NCH, 16]), ALU.mult, [dtt.b, hb.b], [dta.b])

                for g in range(2):
                    wx = wload(Win, 8, 3072 + 512 * g, 512)
                    wbcblk = wload(Win, 8, 4096, 512)
                    zblk = wload(Win, 8, 2048 + 512 * g, 512)

                    def conv_gen(i, Bs, g=g, wx=wx, wbcblk=wbcblk):
                        bx, bcs, dg = Bs["bx"], Bs["bcs"], Bs["dg"]
                        if i < 4:
                            wv, wb, fcol, cidx = wx[0], wx[1], i * 128, 4 * g + i
                        elif i == 4:
                            wv, wb, fcol, cidx = wbcblk[0], wbcblk[1], 128 * g, 8 + g
                        else:
                            wv, wb, fcol, cidx = wbcblk[0], wbcblk[1], 256 + 128 * g, 10 + g
                        if ps_i == 0:
                            C.op("dve", lambda: V.memset(bx.ap[:, 0:3], 0.0), [], [bx.b])
                        else:
                            vcopy(bx.ap[:, 0:3], tailB.ap[:, cidx, :], [tailB.b[cidx]], [bx.b])
                        for kk in range(4):
                            ts(dg.ap[:, kk, :], identB.ap[:], vcol("b_conv_w", kk * 12 + cidx), None, ALU.mult, None, [identB.b, vt.b], [dg.b])

                        def ev_x(pb, pbb, t0, tn):
                            if t0 < TH:
                                acopy(bx.ap[:, 3 + t0:3 + t0 + tn], pb[:, 0:tn], [pbb], [bx.b])
                                if t0 + tn == TH:
                                    acopy(tailB.ap[:, cidx, :], pb[:, tn - 3:tn], [pbb], [tailB.b[cidx]])
                            else:
                                acopy(bxsT.ap[:, cidx, :], pb[:, 0:tn], [pbb], [bxsT.b])
                        proj_fm(wv, wb, fcol, xnT.ap, xnT.b, ncols, ev_x)
                        yield
                        for (t0, tn) in tiles_of(TH):
                            pb, pbb = bank()
                            for kk in range(4):
                                mm(pb[:, 0:tn], dg.ap[:, kk, :], bx.ap[:, t0 + kk:t0 + kk + tn], kk == 0, kk == 3, [dg.b, bx.b], [pbb], kk == 3)
                            act(xbcT.ap[:, i, t0:t0 + tn], pb[:, 0:tn], AF.Silu, [pbb, vt.b], [xbcT.b[i]], bias=vcol("b_conv_b", cidx), scale=1.0)
                        yield
                        if has_s:
                            ts(bcs.ap[:], bxsT.ap[:, cidx, :], vcol("b_conv_w", 3 * 12 + cidx), vcol("b_conv_b", cidx),
                               ALU.mult, ALU.add, [bxsT.b, vt.b], [bcs.b])
                            for kk in range(3):
                                stt(bcs.ap[:], sbcT.ap[:, cidx, kk, :], vcol("b_conv_w", kk * 12 + cidx), bcs.ap[:], ALU.mult, ALU.add,
                                    [sbcT.b, bcs.b, vt.b], [bcs.b])
                            act(xbcT.ap[:, i, TH:TH + NS], bcs.ap[:], AF.Silu, [bcs.b], [xbcT.b[i]])
                            act(xbcsT.ap[:, cidx, :], bcs.ap[:], AF.Silu, [bcs.b], [xbcsT.b])
                            yield

                    run_streams([conv_gen(i, csets[i % 2]) for i in range(6)], skew=SKEW_C)

                    Sg = Sssm.ap[:, 512 * g:512 * (g + 1)]
                    Sgb = Sssm_b[g]
                    acopy(Sb.ap[:], Sg, [Sgb], [Sb.b])

                    def ssd_gen(j, Bs, g=g, Sg=Sg, Sgb=Sgb, zblk=zblk):
                        eall, Rm, Ee, cbm, Wt, xtm, btm = Bs["eall"], Bs["Rm"], Bs["Ee"], Bs["cbm"], Bs["Wt"], Bs["xtm"], Bs["btm"]
                        xdt, xD, xdt2, ytmp, yy, sz, ynb, ssq, stmp = (Bs["xdt"], Bs["xD"], Bs["xdt2"], Bs["ytmp"], Bs["yy"], Bs["sz"],
                                                                       Bs["ynb"], Bs["ssq"], Bs["ytmp"])
                        cs = slice(j * 128, (j + 1) * 128)
                        hs = slice(8 * g, 8 * g + 8)
                        pbt, pbtb = bank()
                        ptb = pbt[:, :].bitcast(BF16)
                        for i in range(4):
                            tp(ptb[:, i * 128:(i + 1) * 128], xbcT.ap[:, i, cs], identB.ap[:], [xbcT.b[i], identB.b], [pbtb], False)
                        tp(ptb[:, 512:640], xbcT.ap[:, 4, cs], identB.ap[:], [xbcT.b[4], identB.b], [pbtb], True)
                        acopy(xtm.ap[:], ptb[:, 0:512], [pbtb], [xtm.b])
                        acopy(btm.ap[:], ptb[:, 512:640], [pbtb], [btm.b])
                        yield
                        pd, pdb = bank()
                        mm(pd[:, 0:8], mle.ap[:], dta.ap[:, j, hs], True, True, [mle.b, dta.b], [pdb], False)
                        mm(pd[:, 16:24], mgt.ap[:], dta.ap[:, j, hs], True, True, [mgt.b, dta.b], [pdb], False)
                        mm(pd[:, 32:40], onesF.ap[:], dta.ap[:, j, hs], True, True, [cbuf, dta.b], [pdb], True)
                        act(eall.ap[:, :, 0:8], pd[:, 0:48].rearrange("p (a h) -> p a h", h=16)[:, :, 0:8], AF.Exp, [pdb], [eall.b])
                        tt(Rm.ap[:], mle.ap[:].unsqueeze(1).to_broadcast([128, 8, 128]),
                           dta.ap[:, j, hs].unsqueeze(2).to_broadcast([128, 8, 128]), ALU.mult, [mle.b, dta.b], [Rm.b])
                        yield
                        for q in range(2):
                            pD, pDb = bank()
                            mm(pD[:, :], mgtB.ap[:], Rm.ap[:, 4 * q:4 * q + 4, :].rearrange("p h t -> p (h t)"), True, True,
                               [mgtB.b, Rm.b], [pDb], True)
                            act(Ee.ap[:, 4 * q:4 * q + 4, :].rearrange("p h t -> p (h t)"), pD[:, :], AF.Exp, [pDb], [Ee.b])
                        yield
                        pc, pcb = bank()
                        mm(pc[:, 0:128], xbcT.ap[:, 4, cs], xbcT.ap[:, 5, cs], True, True, [xbcT.b[4], xbcT.b[5]], [pcb], True)
                        tt(cbm.ap[:], pc[:, 0:128], mle.ap[:], ALU.mult, [pcb, mle.b], [cbm.b])
                        tt(Wt.ap[:], Ee.ap[:], cbm.ap[:].unsqueeze(1).to_broadcast([128, 8, 128]), ALU.mult, [Ee.b, cbm.b], [Wt.b])
                        yield
                        x3 = xtm.ap[:].rearrange("p (h q) -> p h q", q=64)
                        tt(xdt.ap[:].rearrange("p (h q) -> p h q", q=64), x3, dtt.ap[:, j, hs].unsqueeze(2).to_broadcast([128, 8, 64]),
                           ALU.mult, [xtm.b, dtt.b], [xdt.b])
                        tt(xD.ap[:].rearrange("p (h q) -> p h q", q=64), x3, hb.ap[:, 32 + 8 * g:40 + 8 * g].unsqueeze(2).to_broadcast([128, 8, 64]),
                           ALU.mult, [xtm.b, hb.b], [xD.b])
                        tt(xdt2.ap[:].rearrange("p (h q) -> p h q", q=64), xdt.ap[:].rearrange("p (h q) -> p h q", q=64),
                           eall.ap[:, 1, 0:8].unsqueeze(2).to_broadcast([128, 8, 64]), ALU.mult, [xdt.b, eall.b], [xdt2.b])
                        yield
                        py, pyb = bank()
                        mm(py[:, :], identB.ap[:], xD.ap[:], True, False, [identB.b, xD.b], [pyb], False)
                        for h in range(8):
                            mm(py[:, h * 64:(h + 1) * 64], Wt.ap[:, h, :], xdt.ap[:, h * 64:(h + 1) * 64], False, h == 7,
                               [Wt.b, xdt.b], [pyb], h == 7)
                        po, pob = bank()
                        mm(po[:, :], xbcT.ap[:, 5, cs], Sb.ap[:], True, True, [xbcT.b[5], Sb.b], [pob], True)
                        tt(ytmp.ap[:].rearrange("p (h q) -> p h q", q=64), po[:, :].rearrange("p (h q) -> p h q", q=64),
                           eall.ap[:, 0, 0:8].unsqueeze(2).to_broadcast([128, 8, 64]), ALU.mult, [pob, eall.b], [ytmp.b])
                        tt(yy.ap[:], ytmp.ap[:], py[:, :], ALU.add, [ytmp.b, pyb], [yy.b])
                        yield
                        pst, pstb = bank()
                        mm(pst[:, :], btm.ap[:], xdt2.ap[:], True, True, [btm.b, xdt2.b], [pstb], True)
                        tt(stmp.ap[:].rearrange("p (h q) -> p h q", q=64), Sg.rearrange("p (h q) -> p h q", q=64),
                           eall.ap[:, 2, 0:8].unsqueeze(2).to_broadcast([128, 8, 64]), ALU.mult, [Sgb, eall.b], [stmp.b])
                        tt(Sg, stmp.ap[:], pst[:, :], ALU.add, [stmp.b, pstb], [Sgb])
                        acopy(Sb.ap[:], Sg, [Sgb], [Sb.b])
                        yield
                        for _ in gate_norm_gen(g, xnT.ap[:, :, cs], 128, yy, zblk, sz, ssq, ynb, j * 128):
                            yield

                    fill = out_proj_gen(D["ab_w_out"][0:1024, :], ncols) if g == 0 else None
                    run_streams([ssd_gen(j, ssets[j % NS_S]) for j in range(NCH)], skew=SKEW_S, max_active=NS_S, filler=fill, filler_rate=3)
                C.barrier()

        def gate_norm_gen(g, xsrc, M, yy, zb, sz, ssq, ynb, col0):
            pz, pzb = bank()
            for kc in range(8):
                mm(pz[0:M, :], xsrc[:, kc, :], zb[0][:, kc, :], kc == 0, kc == 7, [xnT.b[kc], zb[1]], [pzb], kc == 7)
            act(sz.ap[0:M, :], pz[0:M, :], AF.Silu, [pzb], [sz.b])
            yield
            tt(yy.ap[0:M, :], yy.ap[0:M, :], sz.ap[0:M, :], ALU.mult, [yy.b, sz.b], [yy.b])
            act(sz.ap[0:M, :], yy.ap[0:M, :], AF.Square, [yy.b], [sz.b])
            C.op("dve", lambda: V.tensor_reduce(out=ssq.ap[0:M, 0:1], in_=sz.ap[0:M, :], axis=AX.X, op=ALU.add), [sz.b], [ssq.b])
            yield
            act(ssq.ap[0:M, 1:2], ssq.ap[0:M, 0:1], AF.Ln, [ssq.b], [ssq.b], scale=1.0 / 512.0, bias=EPS)
            act(ssq.ap[0:M, 1:2], ssq.ap[0:M, 1:2], AF.Exp, [ssq.b], [ssq.b], scale=-0.5)
            ts(ynb.ap[0:M, :], yy.ap[0:M, :], ssq.ap[0:M, 1:2], None, ALU.mult, None, [yy.b, ssq.b], [ynb.b])
            yield
            pt_, ptb_ = bank()
            pv = pt_[:, :].bitcast(BF16)
            for i in range(4):
                tp(pv[:, i * 128:i * 128 + M], ynb.ap[0:M, i * 128:(i + 1) * 128], identB.ap[0:M, 0:M], [ynb.b, identB.b], [ptb_], i == 3)
            for i in range(4):
                ci = 4 * g + i
                act(mixT.ap[:, ci, col0:col0 + M], pv[:, i * 128:i * 128 + M], AF.Copy, [ptb_, vt.b], [mixT.b[ci]],
                    scale=vcol("b_norm_w", ci))
            yield

        def gate_norm_out(g, xsrc, M, yy, zb, sz, ssq, ynb, col0):
            for _ in gate_norm_gen(g, xsrc, M, yy, zb, sz, ssq, ynb, col0):
                pass

        axsT = mk("axsT", [128, 8, NS])
        bxsT = mk("bxsT", [128, 12, NS])
        ahsT = mk("ahsT", [128, 8, NS])

        def load_sample_states():
            with ExitStack() as esl:
                s1 = mk("ss1", [16, 3 * 1024], stack=esl)
                s2 = mk("ss2", [16, 3 * 1536], stack=esl)
                s3 = mk("ss3", [16, 1024], stack=esl)
                C.dma("sp", s1.ap[:], D["sac"].rearrange("b k f -> b (k f)"), [], [s1.b])
                C.dma("sp", s2.ap[:], D["sbc"].rearrange("b k f -> b (k f)"), [], [s2.b])
                C.dma("sp", s3.ap[:], D["sah"][:, :], [], [s3.b])
                idn = identF.ap[0:16, 0:16]
                for (src, dst, nchunk, nk, width) in ((s1, sacT, 8, 3, 1024), (s2, sbcT, 12, 3, 1536)):
                    for kk in range(nk):
                        pb, pbb = bank()
                        for c in range(nchunk):
                            tp(pb[:, c * 16:(c + 1) * 16], src.ap[:, kk * width + c * 128:kk * width + (c + 1) * 128], idn,
                               [src.b, identF.b], [pbb], c == nchunk - 1)
                        vcopy(dst.ap[:, :, kk, :], pb[:, 0:nchunk * 16].rearrange("p (c b) -> p c b", b=16), [pbb], [dst.b])
                pb, pbb = bank()
                for c in range(8):
                    tp(pb[:, c * 16:(c + 1) * 16], s3.ap[:, c * 128:(c + 1) * 128], idn, [s3.b, identF.b], [pbb], c == 7)
                vcopy(sahT.ap[:], pb[:, 0:128].rearrange("p (c b) -> p c b", b=16), [pbb], [sahT.b])
                C.dma("sp", O["acs"][:, 0:2, :], D["sac"][:, 1:3, :], [], [])
                C.dma("sp", O["bcs"][:, 0:2, :], D["sbc"][:, 1:3, :], [], [])
                C.barrier()

        def fm_to_rows(src_t, nchunk, ncol, dst_ap, dst_b=None):
            with ExitStack() as esl:
                st = mk("f2r", [16, 1536], stack=esl)
                for c0 in range(0, nchunk, 4):
                    pb, pbb = bank()
                    n = min(4, nchunk - c0)
                    for c in range(n):
                        tp(pb[0:ncol, c * 128:(c + 1) * 128], src_t.ap[:, c0 + c, 0:ncol], identF.ap[:], [src_t.b if not isinstance(src_t.b, list) else src_t.b[c0 + c], identF.b],
                           [pbb], c == n - 1)
                    vcopy(st.ap[0:ncol, c0 * 128:(c0 + n) * 128], pb[0:ncol, 0:n * 128], [pbb], [st.b])
                C.dma("sp", dst_ap, st.ap[0:ncol, 0:nchunk * 128], [st.b], [] if dst_b is None else [dst_b])
                C.barrier()

        def ssd_samples(g):
            with ExitStack() as esl:
                zb = wload(D["ab_w_in"], 8, 2048 + 512 * g, 512)
                yy = mk("s_yy", [16, 512], stack=esl)
                sz = mk("s_sz", [16, 512], stack=esl)
                ynb = mk("s_ynb", [16, 512], BF16, stack=esl)
                ssq = mk("s_ssq", [16, 2], stack=esl)
                tm = mk("s_tm", [16, 1024], stack=esl)
                dts = mk("s_dt", [16, 16], stack=esl)
                dec = mk("s_dec", [16, 16], stack=esl)
                xdtm = mk("s_xdt", [16, 512], BF16, stack=esl)
                decx = mk("s_decx", [16, 512], stack=esl)
                decT = mk("s_decT", [128, 4, 16], stack=esl)
                bcb = mk("s_bcb", [128, 16, 128], stack=esl)
                lms = [mk("s_lm", [16, 16, 128], BF16, stack=esl) for _ in range(2)]
                btmb = mk("s_btm", [16, 128], BF16, stack=esl)
                S0 = [mk(f"s_S0{i}", [128, 16, 128], stack=esl) for i in range(2)]
                t3 = mk("s_t3", [128, 16, 128], stack=esl)
                yT = mk("s_yT", [128, 4, 16], stack=esl)
                oh = mk("s_oh", [16, 16], stack=esl)
                vcopy(oh.ap[:], identF.ap[0:16, 0:16], [identF.b], [oh.b])
                pb, pbb = bank()
                for i in range(4):
                    tp(pb[0:16, i * 128:(i + 1) * 128], xbcsT.ap[:, 4 * g + i, :], identF.ap[:], [xbcsT.b, identF.b], [pbb], False)
                pb2, pbb2 = bank()
                tp(pb2[0:16, 0:128], xbcsT.ap[:, 8 + g, :], identF.ap[:], [xbcsT.b, identF.b], [pbb2], False)
                tp(pb2[0:16, 128:256], xbcsT.ap[:, 10 + g, :], identF.ap[:], [xbcsT.b, identF.b], [pbb2], True)
                vcopy(tm.ap[:, 0:512], pb[0:16, :], [pbb, pbb2], [tm.b])
                vcopy(tm.ap[:, 512:768], pb2[0:16, 0:256], [pbb2], [tm.b])
                pd, pdb = bank()
                for kc in range(8):
                    mm(pd[0:16, 0:16], xnT.ap[:, kc, TH:TH + NS], wdt.ap[:, kc, :], kc == 0, kc == 7, [xnT.b[kc], wdt.b], [pdb], kc == 7)
                tt(dts.ap[:], pd[0:16, 0:16], hb.ap[0:16, 0:16], ALU.add, [pdb, hb.b], [dts.b])
                act(dts.ap[:], dts.ap[:], AF.Exp, [dts.b], [dts.b])
                act(dts.ap[:], dts.ap[:], AF.Ln, [dts.b], [dts.b], bias=1.0, scale=1.0)
                tt(dec.ap[:], dts.ap[:], hb.ap[0:16, 16:32], ALU.mult, [dts.b, hb.b], [dec.b])
                act(dec.ap[:], dec.ap[:], AF.Exp, [dec.b], [dec.b])
                hs = slice(8 * g, 8 * g + 8)
                tt(xdtm.ap[:].rearrange("p (h q) -> p h q", q=64), tm.ap[:, 0:512].rearrange("p (h q) -> p h q", q=64),
                   dts.ap[:, hs].unsqueeze(2).to_broadcast([16, 8, 64]), ALU.mult, [tm.b, dts.b], [xdtm.b])
                vcopy(decx.ap[:].rearrange("p (h q) -> p h q", q=64), dec.ap[:, hs].unsqueeze(2).to_broadcast([16, 8, 64]), [dec.b], [decx.b])
                pb, pbb = bank()
                for i in range(4):
                    tp(pb[:, i * 16:(i + 1) * 16], decx.ap[:, i * 128:(i + 1) * 128], identF.ap[0:16, 0:16], [decx.b, identF.b], [pbb], i == 3)
                vcopy(decT.ap[:], pb[:, 0:64].rearrange("p (i b) -> p i b", b=16), [pbb], [decT.b])
                vcopy(btmb.ap[:], tm.ap[:, 512:640], [tm.b], [btmb.b])
                C.dma("sp", scr_bc[g].rearrange("(b n) -> b n", n=128), tm.ap[:, 640:768], [tm.b], [scr_bc_b])
                C.dma("sp", bcb.ap[:].rearrange("p b n -> p (b n)"), scr_bc[g:g + 1, :].partition_broadcast(128), [scr_bc_b], [bcb.b])
                def ld(i):
                    hp_ = 4 * g + i
                    C.dma("sp", S0[i % 2].ap[:], D["sbs"][:, 2 * hp_:2 * hp_ + 2, :, :].rearrange("b h q n -> (h q) b n"), [], [S0[i % 2].b])
                ld(0)
                for i in range(4):
                    hp = 4 * g + i
                    s0 = S0[i % 2]
                    lm = lms[i % 2]
                    if i + 1 < 4:
                        ld(i + 1)
                    tt(lm.ap[:], xdtm.ap[:, i * 128:(i + 1) * 128].unsqueeze(1).to_broadcast([16, 16, 128]),
                       oh.ap[:].unsqueeze(2).to_broadcast([16, 16, 128]), ALU.mult, [xdtm.b, oh.b], [lm.b])
                    pqs = [bank() for _ in range(4)]
                    for b in range(NS):
                        pq, pqb = pqs[b // 4]
                        mm(pq[:, (b % 4) * 128:(b % 4 + 1) * 128], lm.ap[:, b, :], btmb.ap[:], True, True, [lm.b, btmb.b], [pqb], True)
                    for b in range(NS):
                        act(s0.ap[:, b, :], s0.ap[:, b, :], AF.Copy, [s0.b, decT.b], [s0.b], scale=decT.ap[:, i, b:b + 1])
                    for q4 in range(4):
                        pq, pqb = pqs[q4]
                        tt(s0.ap[:, 4 * q4:4 * q4 + 4, :], s0.ap[:, 4 * q4:4 * q4 + 4, :], pq[:, :].rearrange("p (b v) -> p b v", v=128), ALU.add,
                           [s0.b, pqb], [s0.b])
                    C.dma("sp", O["bss"][:, 2 * hp:2 * hp + 2, :, :].rearrange("b h q n -> (h q) b n"), s0.ap[:], [s0.b], [])
                    tt(t3.ap[:], s0.ap[:], bcb.ap[:], ALU.mult, [s0.b, bcb.b], [t3.b])
                    C.op("dve", lambda: V.tensor_reduce(out=yT.ap[:, i, :], in_=t3.ap[:], axis=AX.X, op=ALU.add), [t3.b], [yT.b])
                pb, pbb = bank()
                for i in range(4):
                    tp(pb[0:16, i * 128:(i + 1) * 128], yT.ap[:, i, :], identF.ap[:], [yT.b, identF.b], [pbb], i == 3)
                tt(tm.ap[:, 0:512].rearrange("p (h q) -> p h q", q=64), tm.ap[:, 0:512].rearrange("p (h q) -> p h q", q=64),
                   hb.ap[0:16, 32 + 8 * g:40 + 8 * g].unsqueeze(2).to_broadcast([16, 8, 64]), ALU.mult, [tm.b, hb.b], [tm.b])
                tt(yy.ap[0:16, :], tm.ap[:, 0:512], pb[0:16, :], ALU.add, [tm.b, pbb], [yy.b])
                gate_norm_out(g, xnT.ap[:, :, TH:TH + NS], NS, yy, zb, sz, ssq, ynb, TH)
                C.barrier()

        def layer1(ps_i, ncols):
            Win = D["c_w_in"]
            has_s = ncols > TH
            NCH = TH // 128
            n = ncols
            with ExitStack() as esl:
                vtm = mk("vtm", [128, NCH, 512], BF16, stack=esl)
                vs = mk("c_vs", [16, 512], BF16, stack=esl)
                hsb = hgrn_alloc(esl) if has_s else None
                HC = 4
                sets = []
                nsets = NS_H if not has_s else 2
                for i in range(nsets):
                    sets.append(dict(
                        sg=mk("c_sg", [128, NCOL], stack=esl), gg=mk("c_g", [128, NCOL], stack=esl), Bc=mk("c_B", [128, NCOL], stack=esl),
                        qsc=mk("c_qsc", [128, NS], stack=esl), qraw=mk("c_qraw", [128, NCOL], BF16, stack=esl),
                        sgate=mk("c_sgate", [128, NCOL], BF16, stack=esl), qt=mk("c_qt", [128, TH], BF16, stack=esl),
                        kt=mk("c_kt", [128, TH], BF16, stack=esl),
                        AtA=mk("c_At", [128, HC, 128], BF16, stack=esl), ktmA=mk("c_ktm", [128, HC, 128], BF16, stack=esl),
                        kvA=mk("c_kv", [128, HC, 128], F32, stack=esl), SbfA=mk("c_Sbf", [128, HC, 128], BF16, nb=HC, stack=esl),
                        sc1=mk("c_sc", [128, 16], stack=esl), sc2=mk("c_sc2", [128, 24], stack=esl), sce=mk("c_sce", [128, 24], stack=esl)))
                def head_gen(h, fc, wq, wf, wg, Bs):
                    sg, gg, Bc, qsc, qt, kt = Bs["sg"], Bs["gg"], Bs["Bc"], Bs["qsc"], Bs["qt"], Bs["kt"]
                    qraw, sgate = Bs["qraw"], Bs["sgate"]
                    AtA, ktmA, kvA, SbfA, sc1, sc2, sce = Bs["AtA"], Bs["ktmA"], Bs["kvA"], Bs["SbfA"], Bs["sc1"], Bs["sc2"], Bs["sce"]
                    oT = gg
                    o2 = sg
                    lbc = dv.ap[:, 16 + h:17 + h]
                    omc = dv.ap[:, 32 + h:33 + h]
                    nomc = dv.ap[:, 48 + h:49 + h]
                    qscale = float(128 ** -0.5)

                    if has_s:
                        for hf_ in range(2):
                            bs_ = slice(hf_ * (NS // 2), (hf_ + 1) * (NS // 2))
                            C.dma("sp", hsb["S0hs"][h % 2][hf_].ap[:], D["sc"][bs_, h, :, :].rearrange("b k v -> k b v"), [],
                                  [hsb["S0hs"][h % 2][hf_].b])

                    def ev_f(pb, pbb, t0, tn):
                        act(sg.ap[:, t0:t0 + tn], pb[:, 0:tn], AF.Sigmoid, [pbb], [sg.b])
                    proj_fm(wf[0], wf[1], fc, xnT.ap, xnT.b, ncols, ev_f)
                    yield

                    def ev_q(pb, pbb, t0, tn):
                        if t0 < TH:
                            acopy(qraw.ap[:, t0:t0 + tn], pb[:, 0:tn], [pbb], [qraw.b])
                        else:
                            C.op("act", lambda: S_.mul(out=qsc.ap[:, 0:tn], in_=pb[:, 0:tn], mul=qscale), [pbb], [qsc.b])
                    proj_fm(wq[0], wq[1], fc, xnT.ap, xnT.b, ncols, ev_q)
                    yield

                    def ev_g(pb, pbb, t0, tn):
                        act(sgate.ap[:, t0:t0 + tn], pb[:, 0:tn], AF.Silu, [pbb], [sgate.b])
                    proj_fm(wg[0], wg[1], fc, xnT.ap, xnT.b, ncols, ev_g)
                    yield
                    act(gg.ap[:, 0:n], sg.ap[:, 0:n], AF.Ln, [sg.b, dv.b], [gg.b], scale=omc, bias=lbc)
                    ts(sg.ap[:, 0:n], sg.ap[:, 0:n], nomc, omc, ALU.mult, ALU.add, [sg.b, dv.b], [sg.b])
                    C.op("dve", lambda: V.tensor_tensor_scan(out=Bc.ap[:, 0:TH], data0=onesF.ap[:, 0:1].to_broadcast([128, TH]),
                                                             data1=gg.ap[:, 0:TH], initial=0.0, op0=ALU.mult, op1=ALU.add),
                         [cbuf, gg.b], [Bc.b])
                    yield
                    B3 = Bc.ap[:, 0:TH].rearrange("p (j t) -> p j t", t=128)
                    vcopy(sc1.ap[:, 0:8], B3[:, :, 63], [Bc.b], [sc1.b])
                    vcopy(sc1.ap[:, 8:16], B3[:, :, 127], [Bc.b], [sc1.b])
                    tt(gg.ap[:, 0:TH].rearrange("p (j t) -> p j t", t=128), B3, sc1.ap[:, 0:8].unsqueeze(2).to_broadcast([128, NCH, 128]),
                       ALU.subtract, [Bc.b, sc1.b], [gg.b])
                    vcopy(sc2.ap[:, 0:1], sc1.ap[:, 0:1], [sc1.b], [sc2.b])
                    tt(sc2.ap[:, 1:8], sc1.ap[:, 1:8], sc1.ap[:, 8:15], ALU.subtract, [sc1.b], [sc2.b])
                    vcopy(sc2.ap[:, 8:9], sc1.ap[:, 8:9], [sc1.b], [sc2.b])
                    tt(sc2.ap[:, 9:16], sc1.ap[:, 9:16], sc1.ap[:, 8:15], ALU.subtract, [sc1.b], [sc2.b])
                    tt(sc2.ap[:, 16:24], sc1.ap[:, 8:16], sc1.ap[:, 0:8], ALU.subtract, [sc1.b], [sc2.b])
                    yield
                    act(sce.ap[:], sc2.ap[:], AF.Exp, [sc2.b], [sce.b])
                    act(Bc.ap[:, 0:TH], gg.ap[:, 0:TH], AF.Exp, [gg.b], [Bc.b])
                    stt(qt.ap[:, 0:TH], qraw.ap[:, 0:TH], qscale, Bc.ap[:, 0:TH], ALU.mult, ALU.mult, [qraw.b, Bc.b], [qt.b])
                    yield
                    act(Bc.ap[:, 0:TH], gg.ap[:, 0:TH], AF.Exp, [gg.b, qt.b], [Bc.b], scale=-1.0)
                    tt(kt.ap[:, 0:TH], sg.ap[:, 0:TH], Bc.ap[:, 0:TH], ALU.mult, [sg.b, Bc.b], [kt.b])
                    yield
                    for hf in range(NCH // HC):
                        c0 = hf * HC * 128
                        pa, pab = bank()
                        for jj in range(HC):
                            cs = slice(c0 + jj * 128, c0 + (jj + 1) * 128)
                            mm(pa[:, jj * 128:(jj + 1) * 128], kt.ap[:, cs], qt.ap[:, cs], True, True, [kt.b, qt.b], [pab], True)
                        tt(AtA.ap[:], pa[:, 0:HC * 128].rearrange("p (j t) -> p j t", t=128), mle.ap[:].unsqueeze(1).to_broadcast([128, HC, 128]),
                           ALU.mult, [pab, mle.b], [AtA.b])
                        pk, pkb = bank()
                        pkv = pk[:, :].bitcast(BF16)
                        for jj in range(HC):
                            cs = slice(c0 + jj * 128, c0 + (jj + 1) * 128)
                            tp(pkv[:, jj * 128:(jj + 1) * 128], kt.ap[:, cs], identB.ap[:], [kt.b, identB.b], [pkb], True)
                        acopy(ktmA.ap[:], pkv[:, 0:HC * 128].rearrange("p (j t) -> p j t", t=128), [pkb], [ktmA.b])
                        yield
                        pS, pSb = bank()
                        for jj in range(HC):
                            j = hf * HC + jj
                            mm(pS[:, jj * 128:(jj + 1) * 128], ktmA.ap[:, jj, :], vtm.ap[:, j, fc:fc + 128], True, True, [ktmA.b, vtm.b], [pSb], True)
                        acopy(kvA.ap[:], pS[:, 0:HC * 128].rearrange("p (j t) -> p j t", t=128), [pSb], [kvA.b])
                        yield
                        for jj in range(HC):
                            j = hf * HC + jj
                            ts(SbfA.ap[:, jj, :], Sc.ap[:, h, :], sce.ap[:, j:j + 1], None, ALU.mult, None, [Sc.b[h], sce.b], [SbfA.b[jj]])
                            ts(Sc.ap[:, h, :], Sc.ap[:, h, :], sce.ap[:, 8 + j:9 + j], None, ALU.mult, None, [Sc.b[h], sce.b], [Sc.b[h]])
                            stt(Sc.ap[:, h, :], kvA.ap[:, jj, :], sce.ap[:, 16 + j:17 + j], Sc.ap[:, h, :], ALU.mult, ALU.add,
                                [kvA.b, sce.b, Sc.b[h]], [Sc.b[h]])
                        yield
                        po, pob = bank()
                        for jj in range(HC):
                            j = hf * HC + jj
                            cs = slice(c0 + jj * 128, c0 + (jj + 1) * 128)
                            mm(po[:, jj * 128:(jj + 1) * 128], vtm.ap[:, j, fc:fc + 128], AtA.ap[:, jj, :], True, False, [vtm.b, AtA.b], [pob], False)
                            mm(po[:, jj * 128:(jj + 1) * 128], SbfA.ap[:, jj, :], qt.ap[:, cs], False, True, [SbfA.b[jj], qt.b], [pob], True)
                        acopy(oT.ap[:, c0:c0 + HC * 128], po[:, 0:HC * 128], [pob], [oT.b])
                        yield
                    if has_s:
                        hgrn_samples(h, fc, sg, gg, qsc, vs, oT, hsb)
                        yield
                    act(qraw.ap[:, 0:n], oT.ap[:, 0:n], AF.Square, [oT.b], [qraw.b])
                    for (t0, tn) in tiles_of(ncols):
                        pb, pbb = bank()
                        mm(pb[:, 0:tn], onesB.ap[:], qraw.ap[:, t0:t0 + tn], True, True, [onesB.b, qraw.b], [pbb], True)
                        act(o2.ap[:, t0:t0 + tn], pb[:, 0:tn], AF.Ln, [pbb], [o2.b], scale=1.0 / 128.0, bias=EPS)
                    yield
                    act(o2.ap[:, 0:n], o2.ap[:, 0:n], AF.Exp, [o2.b], [o2.b], scale=-0.5)
                    stt(oT.ap[:, 0:n], oT.ap[:, 0:n], vcol("c_norm_w", h), o2.ap[:, 0:n], ALU.mult, ALU.mult, [oT.b, o2.b, vt.b], [oT.b])
                    tt(mixT.ap[:, h % 8, 0:n], oT.ap[:, 0:n], sgate.ap[:, 0:n], ALU.mult, [oT.b, sgate.b], [mixT.b[h % 8]])
                    yield

                for h4 in range(4):
                    wv = wload(Win, 8, 4096 + 512 * h4, 512)
                    wq = wload(Win, 8, 512 * h4, 512)
                    wf = wload(Win, 8, 2048 + 512 * h4, 512)
                    wg = wload(Win, 8, 6144 + 512 * h4, 512)
                    for j in range(NCH):
                        pb, pbb = bank()
                        for kc in range(8):
                            mm(pb[:, :], xnT.ap[:, kc, j * 128:(j + 1) * 128], wv[0][:, kc, :], kc == 0, kc == 7, [xnT.b[kc], wv[1]], [pbb], kc == 7)
                        acopy(vtm.ap[:, j, :], pb[:, :], [pbb], [vtm.b])
                    if has_s:
                        pb, pbb = bank()
                        for kc in range(8):
                            mm(pb[0:16, :], xnT.ap[:, kc, TH:TH + NS], wv[0][:, kc, :], kc == 0, kc == 7, [xnT.b[kc], wv[1]], [pbb], kc == 7)
                        vcopy(vs.ap[:, :], pb[0:16, :], [pbb], [vs.b])
                    fill = out_proj_gen(D["c_w_out"][0:1024, :], ncols) if h4 == 2 else None
                    run_streams([head_gen(4 * h4 + i, i * 128, wq, wf, wg, sets[i % nsets]) for i in range(4)], skew=SKEW_H, max_active=nsets,
                                filler=fill)
                    if h4 == 3:
                        out_proj(D["c_w_out"][1024:2048, :], ncols)
                C.barrier()

        def hgrn_alloc(esl):
            return dict(eg=mk("hs_eg", [128, NS], stack=esl), ktmS=mk("hs_ktm", [16, 128], BF16, stack=esl),
                        lm=mk("hs_lm", [128, 16, 128], BF16, stack=esl),
                        S0hs=[[mk("hs_S0a", [128, 8, 128], stack=esl), mk("hs_S0b", [128, 8, 128], stack=esl)] for _ in range(2)],
                        qb=mk("hs_qb", [128, NS], BF16, stack=esl),
                        od=mk("hs_od", [128, 16, 16], stack=esl), oh=mk("hs_oh", [16, 16], stack=esl))

        def hgrn_samples(h, fc, sg, gg, qsc, vs, oT, hsb):
            sl = slice(TH, TH + NS)
            eg, ktmS, lm, od, oh, qb = hsb["eg"], hsb["ktmS"], hsb["lm"], hsb["od"], hsb["oh"], hsb["qb"]
            vcopy(oh.ap[:], identF.ap[0:16, 0:16], [identF.b], [oh.b])
            act(eg.ap[:], gg.ap[:, sl], AF.Exp, [gg.b], [eg.b])
            pb, pbb = bank()
            tp(pb[0:16, 0:128], sg.ap[:, sl], identF.ap[:], [sg.b, identF.b], [pbb], True)
            vcopy(ktmS.ap[:], pb[0:16, 0:128], [pbb], [ktmS.b])
            HB = NS // 2
            S0h = hsb["S0hs"][h % 2]
            tt(lm.ap[0:16], ktmS.ap[:].unsqueeze(1).to_broadcast([16, 16, 128]), oh.ap[:].unsqueeze(2).to_broadcast([16, 16, 128]),
               ALU.mult, [ktmS.b, oh.b], [lm.b])
            pqs = [bank() for _ in range(4)]
            for b in range(NS):
                pq, pqb = pqs[b // 4]
                mm(pq[:, (b % 4) * 128:(b % 4 + 1) * 128], lm.ap[0:16, b, :], vs.ap[:, (h % 4) * 128:(h % 4 + 1) * 128], True, True, [lm.b, vs.b], [pqb], True)
            for hf in range(2):
                sh = S0h[hf]
                for bb in range(HB):
                    b = hf * HB + bb
                    act(sh.ap[:, bb, :], sh.ap[:, bb, :], AF.Copy, [sh.b, eg.b], [sh.b], scale=eg.ap[:, b:b + 1])
                for q4 in range(HB // 4):
                    pq, pqb = pqs[hf * (HB // 4) + q4]
                    tt(sh.ap[:, 4 * q4:4 * q4 + 4, :], sh.ap[:, 4 * q4:4 * q4 + 4, :], pq[:, :].rearrange("p (b v) -> p b v", v=128), ALU.add,
                       [sh.b, pqb], [sh.b])
                bs = slice(hf * HB, (hf + 1) * HB)
                C.dma("sp", O["cs"][bs, h, :, :].rearrange("b k v -> k b v"), sh.ap[:], [sh.b], [])
            for hf in range(2):
                (acopy if hf == 0 else vcopy)(lm.ap[:, hf * HB:(hf + 1) * HB, :], S0h[hf].ap[:], [S0h[hf].b], [lm.b])
            acopy(qb.ap[:], qsc.ap[:], [qsc.b], [qb.b])
            po, pob = bank()
            for b in range(NS):
                mm(po[:, b * 16:(b + 1) * 16], lm.ap[:, b, :], qb.ap[:], True, True, [lm.b, qb.b], [pob], b == NS - 1)
            tt(od.ap[:], po[:, 0:256].rearrange("p (b c) -> p b c", c=16), identBC.ap[:], ALU.mult, [pob, identBC.b], [od.b])
            C.op("dve", lambda: V.tensor_reduce(out=oT.ap[:, sl], in_=od.ap[:], axis=AX.X, op=ALU.add), [od.b], [oT.b])

        identBC = mk("identBC", [128, 16, 16])
        C.op("pool", lambda: G.memset(identBC.ap[:], 1.0), [], [identBC.b])
        C.op("pool", lambda: G.affine_select(out=identBC.ap[:], in_=identBC.ap[:], pattern=[[1, 16], [-1, 16]], compare_op=ALU.is_equal,
                                             fill=0.0, base=0, channel_multiplier=0), [identBC.b], [identBC.b])

        def store_out(ps_i, ncols, normed):
            with ExitStack() as esl:
                if normed:
                    rmsnorm(ncols, "norm_f", 0, hT)
                src = hT
                yo = [mk(f"yo{i}", [128, 1024], stack=esl) for i in range(2)]
                for j in range(TH // 128):
                    y = yo[j % 2]
                    for c0 in (0, 4):
                        pb, pbb = bank()
                        for c in range(4):
                            tp(pb[:, c * 128:(c + 1) * 128], src.ap[:, c0 + c, j * 128:(j + 1) * 128], identF.ap[:], [src.b[c0 + c], identF.b],
                               [pbb], c == 3)
                        if c0 == 0:
                            acopy(y.ap[:, 0:512], pb[:, :], [pbb], [y.b])
                        else:
                            vcopy(y.ap[:, 512:1024], pb[:, :], [pbb], [y.b])
                    r0 = ps_i * TH + j * 128
                    C.dma("sp", O["y_p"][r0:r0 + 128, :], y.ap[:], [y.b], [])
                if ncols > TH:
                    y = yo[0]
                    for c0 in (0, 4):
                        pb, pbb = bank()
                        for c in range(4):
                            tp(pb[0:16, c * 128:(c + 1) * 128], src.ap[:, c0 + c, TH:TH + NS], identF.ap[:], [src.b[c0 + c], identF.b], [pbb], c == 3)
                        vcopy(y.ap[0:16, c0 * 128:c0 * 128 + 512], pb[0:16, :], [pbb], [y.b])
                    C.dma("sp", O["y_s"][:, :], y.ap[0:16, :], [y.b], [])
                C.barrier()

        es_s = ExitStack()
        sacT = mk("sacT", [128, 8, 3, NS], stack=es_s)
        sbcT = mk("sbcT", [128, 12, 3, NS], stack=es_s)
        sahT = mk("sahT", [128, 8, NS], stack=es_s)
        xbcsT = mk("xbcsT", [128, 12, NS], stack=es_s)
        load_sample_states()
        for ps_i in range(2):
            ncols = NCOL if ps_i == 0 else TH
            phase0(ps_i, ncols)
            if DBG_STOP >= 1:
                if "N" in DBG_PARTS:
                    rmsnorm(ncols, "norm_w", 0, xnT)
                if "A" in DBG_PARTS:
                    layer0_A(ps_i, ncols)
                if "B" in DBG_PARTS:
                    layer0_B(ps_i, ncols)
                if ncols > TH and "S" in DBG_PARTS:
                    ssd_samples(0)
                    ssd_samples(1)
                if ps_i == 0:
                    C.barrier()
                    es_s.close()
                if "O" in DBG_PARTS:
                    out_proj(D["ab_w_out"][1024:2048, :], ncols)
                    ple(0, ps_i, ncols)
            if DBG_STOP >= 2:
                rmsnorm(ncols, "norm_w", 8, xnT)
                layer1(ps_i, ncols)
                ple(1, ps_i, ncols)
            store_out(ps_i, ncols, DBG_STOP >= 2)
        if "D" in DBG_PARTS:
            C.dma("sp", O["y_p"][0:128, 0:64], dv.ap[:, :], [dv.b], [])
            C.dma("sp", O["y_p"][0:128, 64:320], vt.ap[:, :], [vt.b], [])
            C.dma("sp", O["y_p"][0:128, 320:328], hlast.ap[:, :], [hlast.b], [])
        hl3 = mk("hl3", [128, 8, 1])
        vcopy(hl3.ap[:, :, 0], hlast.ap[:, :], [hlast.b], [hl3.b])
        fm_to_rows(hl3, 8, 1, O["ahp"].rearrange("(o f) -> o f", o=1))
        fm_to_rows(tailA, 8, 3, O["acp"][:, :])
        fm_to_rows(tailB, 12, 3, O["bcp"][:, :])
        with ExitStack() as esl:
            so = mk("so", [128, 8, 128], stack=esl)
            for c0 in (0, 4):
                pb, pbb = bank()
                for c in range(4):
                    tp(pb[:, c * 128:(c + 1) * 128], Sssm.ap[:, (c0 + c) * 128:(c0 + c + 1) * 128], identF.ap[:], [Sssm_b[(c0 + c) // 4], identF.b],
                       [pbb], c == 3)
                vcopy(so.ap[:, c0:c0 + 4, :], pb[:, :].rearrange("p (c n) -> p c n", n=128), [pbb], [so.b])
            C.dma("sp", O["bsp"].rearrange("(c h2) q n -> (h2 q) c n", h2=2), so.ap[:], [so.b], [])
            C.dma("sp", O["cp"].rearrange("h k v -> k h v"), Sc.ap[:], Sc.b, [])
            fm_to_rows(ahsT, 8, NS, O["ahs"][:, :])
            fm_to_rows(axsT, 8, NS, O["acs"][:, 2, :])
            fm_to_rows(bxsT, 12, NS, O["bcs"][:, 2, :])
        C.barrier(engines=("sp",), with_pool=True)
        _NC_CACHE["marks"] = C.marks
        _NC_CACHE["nwait"] = C.nwait
    return nc


IN_SHAPES = {
    "xp": [2048, 1024], "xs": [16, 1024], "pp": [2, 2048, 256], "psm": [2, 16, 256],
    "sah": [16, 1024], "sac": [16, 3, 1024], "sbs": [16, 16, 64, 128], "sbc": [16, 3, 1536], "sc": [16, 16, 128, 128],
    "norm_w": [2, 1024], "norm_f": [1024], "ab_w_in": [1024, 4624], "a_conv_w": [4, 1024], "a_conv_b": [1024],
    "a_w_r": [8, 128, 128], "a_b_r": [1024], "a_w_i": [8, 128, 128], "a_b_i": [1024], "a_lam": [1024],
    "b_conv_w": [4, 1536], "b_conv_b": [1536], "b_dt_bias": [16], "b_a_log": [16], "b_d": [16], "b_norm_w": [1024],
    "ab_w_out": [2048, 1024], "c_w_in": [1024, 8192], "c_lb": [2, 2048], "c_norm_w": [2048], "c_w_out": [2048, 1024],
    "ple_proj": [2, 256, 1024], "ple_gate": [2, 1024, 1024],
}
OUT_SHAPES = {
    "y_p": [2048, 1024], "y_s": [16, 1024], "ahp": [1024], "acp": [3, 1024], "bsp": [16, 64, 128], "bcp": [3, 1536],
    "cp": [16, 128, 128], "ahs": [16, 1024], "acs": [16, 3, 1024], "bss": [16, 16, 64, 128], "bcs": [16, 3, 1536],
    "cs": [16, 16, 128, 128],
}

_NC_CACHE = {}


def _in_maps(inputs, n=8):
    f = lambda a: np.ascontiguousarray(np.asarray(a, dtype=np.float32))
    I = {k: np.asarray(v) for k, v in inputs.items()}
    shared = {
        "norm_w": I["norm_w"], "norm_f": I["norm_f"], "ab_w_in": I["ab_w_in"][0], "a_conv_w": I["a_conv_w"][0],
        "a_conv_b": I["a_conv_b"][0], "a_w_r": I["a_w_r"][0], "a_b_r": I["a_b_r"][0], "a_w_i": I["a_w_i"][0],
        "a_b_i": I["a_b_i"][0], "a_lam": I["a_lam"][0], "b_conv_w": I["b_conv_w"][0], "b_conv_b": I["b_conv_b"][0],
        "b_dt_bias": I["b_dt_bias"][0], "b_a_log": I["b_a_log"][0], "b_d": I["b_d"][0], "b_norm_w": I["b_norm_w"][0],
        "ab_w_out": I["ab_w_out"][0], "c_w_in": I["c_w_in"][0], "c_lb": I["c_lb"], "c_norm_w": I["c_norm_w"][0],
        "c_w_out": I["c_w_out"][0], "ple_proj": I["ple_proj"], "ple_gate": I["ple_gate"],
    }
    shared = {k: f(v) for k, v in shared.items()}
    maps = []
    for c in range(n):
        s = slice(16 * c, 16 * c + 16)
        m = dict(shared)
        m["xp"] = f(I["x_prompt"][c])
        m["xs"] = f(I["x_sample"][s, 0])
        m["pp"] = f(I["p_prompt"][:, c])
        m["psm"] = f(I["p_sample"][:, s, 0])
        m["sah"] = f(I["state_a_h"][0, s])
        m["sac"] = f(I["state_a_conv"][0, s])
        m["sbs"] = f(I["state_b_ssm"][0, s])
        m["sbc"] = f(I["state_b_conv"][0, s])
        m["sc"] = f(I["state_c"][0, s])
        maps.append(m)
    return maps


def _assemble(results):
    cat = lambda k, ax=0: np.concatenate([np.asarray(r[k], dtype=np.float32) for r in results], axis=ax)
    stk = lambda k: np.stack([np.asarray(r[k], dtype=np.float32) for r in results], axis=0)
    y_p = stk("y_p")
    y_s = cat("y_s")[:, None, :]
    return (y_p, y_s,
            stk("ahp")[None], stk("acp")[None], stk("bsp")[None], stk("bcp")[None], stk("cp")[None],
            cat("ahs")[None], cat("acs")[None], cat("bss")[None], cat("bcs")[None], cat("cs")[None])


def kernel(**inputs):
    if "nc" not in _NC_CACHE:
        _NC_CACHE["nc"] = build_program()
    nc = _NC_CACHE["nc"]
    maps = _in_maps(inputs)
    res = run_bass_kernel_spmd(nc, maps, core_ids=list(range(8)))
    return _assemble(res.results)
```

```python
import numpy as np
import concourse.bass as bass
import concourse.mybir as mybir
from concourse.bass_utils import run_bass_kernel_spmd
from contextlib import ExitStack

F32, BF16 = mybir.dt.float32, mybir.dt.bfloat16
AF = mybir.ActivationFunctionType
ALU = mybir.AluOpType
AX = mybir.AxisListType

T_ALL = 2048
TH = 1024
NS = 16
NCOL = TH + NS
EPS = 1e-6
DBG_STOP = 2
DBG_PARTS = "NABSO"
import os as _os
SKEW_A = int(_os.environ.get("K_SKEW_A", "1"))
SKEW_C = int(_os.environ.get("K_SKEW_C", "1"))
SKEW_S = int(_os.environ.get("K_SKEW_S", "1"))
SKEW_H = int(_os.environ.get("K_SKEW_H", "1"))
NS_A = int(_os.environ.get("K_NS_A", "2"))
NS_S = int(_os.environ.get("K_NS_S", "2"))
NS_H = int(_os.environ.get("K_NS_H", "2"))


class Buf:
    __slots__ = ("name", "w", "r")

    def __init__(self, name):
        self.name = name
        self.w = None
        self.r = {}


class Ctx:
    R = 8

    def __init__(self, nc, es):
        self.nc = nc
        self.eng = {"pe": nc.tensor, "act": nc.scalar, "dve": nc.vector, "pool": nc.gpsimd, "sp": nc.sync}
        self.sem = {e: es.enter_context(nc.semaphore("s_" + e)) for e in ("pe", "act", "dve", "pool")}
        self.cnt = {e: 0 for e in self.sem}
        self.known = {e: {} for e in self.eng}
        self.RQ = {"sp": 8, "pool": 3}
        self.ring = {q: [es.enter_context(nc.semaphore(f"d_{q}{i}")) for i in range(self.RQ[q])] for q in ("sp", "pool")}
        self.rcnt = {q: [0] * self.RQ[q] for q in self.ring}
        self.ridx = {q: 0 for q in self.ring}
        self.nwait = 0
        self.marks = []

    def _wait(self, e, ev):
        key, h, v = ev
        if self.known[e].get(key, 0) >= v:
            return
        self.eng[e].wait_ge(h, v)
        self.known[e][key] = v
        self.nwait += 1

    def _deps(self, e, reads, writes):
        for b in reads:
            if b.w is not None:
                if not (b.w[0] == "pe" and e == "pe"):
                    self._wait(e, b.w)
        for b in writes:
            if b.w is not None:
                if not (b.w[0] == "pe" and e == "pe"):
                    self._wait(e, b.w)
            for k, ev in b.r.items():
                if ev[0] == e and e == "pe":
                    continue
                self._wait(e, ev)

    def _post(self, ev, reads, writes):
        for b in reads:
            b.r[ev[0]] = ev
        for b in writes:
            b.w = ev
            b.r = {}

    def op(self, e, fn, reads=(), writes=(), inc=True):
        self._deps(e, reads, writes)
        inst = fn()
        if inc:
            self.cnt[e] += 1
            assert self.cnt[e] < 60000
            inst.then_inc(self.sem[e], 1)
            ev = (e, self.sem[e], self.cnt[e])
        else:
            ev = (e, self.sem[e], self.cnt[e] + 1)
        self._post(ev, reads, writes)
        return inst

    def dma(self, q, out, in_, reads=(), writes=()):
        self._deps(q, reads, writes)
        i = self.ridx[q] % self.RQ[q]
        self.ridx[q] += 1
        if self.rcnt[q][i] > 0:
            self._wait(q, (f"dma_{q}{i}", self.ring[q][i], self.rcnt[q][i]))
        self.rcnt[q][i] += 16
        assert self.rcnt[q][i] < 60000
        inst = self.eng[q].dma_start(out=out, in_=in_)
        inst.then_inc(self.ring[q][i], 16)
        ev = (f"dma_{q}{i}", self.ring[q][i], self.rcnt[q][i])
        self._post(ev, reads, writes)

    def all_events(self):
        evs = [(e, self.sem[e], self.cnt[e]) for e in self.sem if self.cnt[e] > 0]
        for q in self.ring:
            for i in range(self.RQ[q]):
                if self.rcnt[q][i] > 0:
                    evs.append((f"dma_{q}{i}", self.ring[q][i], self.rcnt[q][i]))
        return evs

    def barrier(self, engines=("act", "dve", "sp"), with_pool=False):
        import inspect
        self.marks.append((inspect.stack()[1].function, dict(self.cnt)))
        evs = self.all_events()
        if not with_pool:
            evs = [ev for ev in evs if ev[0] != "pool" and not ev[0].startswith("dma_pool")]
        for e in engines:
            for ev in evs:
                if ev[0] == e:
                    continue
                self._wait(e, ev)


class T:
    def __init__(self, nc, es, name, shape, dtype=F32, nb=None):
        self.ap = es.enter_context(nc.sbuf_tensor(name, list(shape), dtype))
        if nb is None:
            self.b = Buf(name)
        else:
            self.b = [Buf(f"{name}{i}") for i in range(nb)]


def build_program():
    nc = bass.Bass("TRN2", target_bir_lowering=False)

    def din(name, shape):
        return nc.dram_tensor(name, list(shape), F32, kind="ExternalInput").ap()

    def dout(name, shape):
        return nc.dram_tensor(name, list(shape), F32, kind="ExternalOutput").ap()

    D = {}
    for name, shape in IN_SHAPES.items():
        D[name] = din(name, shape)
    O = {}
    for name, shape in OUT_SHAPES.items():
        O[name] = dout(name, shape)
    scr_bc = nc.dram_tensor("scr_bc", [2, 2048], F32, kind="Internal").ap()
    scr_bc_b = Buf("scr_bc")

    with ExitStack() as es:
        C = Ctx(nc, es)
        V, S_, G = nc.vector, nc.scalar, nc.gpsimd

        uid = {"n": 0}

        def mk(name, shape, dtype=F32, nb=None, stack=es):
            uid["n"] += 1
            return T(nc, stack, f"{name}_{uid['n']}", shape, dtype, nb)

        def act(out, in_, func, R, W, **kw):
            return C.op("act", lambda: S_.activation(out=out, in_=in_, func=func, **kw), R, W)

        def acopy(out, in_, R, W):
            return C.op("act", lambda: S_.copy(out=out, in_=in_), R, W)

        def vcopy(out, in_, R, W):
            return C.op("dve", lambda: V.tensor_copy(out=out, in_=in_), R, W)

        def tt(out, in0, in1, op, R, W):
            return C.op("dve", lambda: V.tensor_tensor(out=out, in0=in0, in1=in1, op=op), R, W)

        def ts(out, in0, s1, s2, op0, op1, R, W):
            if s2 is None:
                return C.op("dve", lambda: V.tensor_scalar(out=out, in0=in0, scalar1=s1, scalar2=None, op0=op0), R, W)
            return C.op("dve", lambda: V.tensor_scalar(out=out, in0=in0, scalar1=s1, scalar2=s2, op0=op0, op1=op1), R, W)

        def stt(out, in0, scalar, in1, op0, op1, R, W):
            return C.op("dve", lambda: V.scalar_tensor_tensor(out=out, in0=in0, scalar=scalar, in1=in1, op0=op0, op1=op1), R, W)

        def mm(out, lhsT, rhs, start, stop, R, W, inc):
            return C.op("pe", lambda: nc.tensor.matmul(out, lhsT=lhsT, rhs=rhs, start=start, stop=stop), R, W, inc=True)

        def tp(out, in_, ident, R, W, inc):
            return C.op("pe", lambda: nc.tensor.transpose(out, in_, ident), R, W, inc=True)

        banks = [es.enter_context(nc.psum_tensor(f"pb{i}", [128, 512], F32)) for i in range(8)]
        bbuf = [Buf(f"pb{i}") for i in range(8)]
        bstate = {"i": 0}

        def bank():
            i = bstate["i"] % 8
            bstate["i"] += 1
            return banks[i], bbuf[i]

        NW = 5
        wpool = [mk(f"wb{i}", [128, 4096], BF16) for i in range(NW)]
        wstate = {"i": 0}

        def wload(Wap, kc, col0, ncols):
            t = wpool[wstate["i"] % NW]
            wstate["i"] += 1
            view = t.ap[:, 0:kc * ncols].rearrange("p (k f) -> p k f", k=kc)
            src = Wap.rearrange("(k p) f -> p k f", p=128)[:, :, col0:col0 + ncols]
            C.dma("pool", view, src, reads=[], writes=[t.b])
            return view, t.b

        identF = mk("identF", [128, 128])
        identB = mk("identB", [128, 128], BF16)
        mle = mk("mle", [128, 128])
        mgt = mk("mgt", [128, 128])
        onesF = mk("onesF", [128, 128])
        cbuf = Buf("consts")

        C.op("pool", lambda: G.memset(onesF.ap[:], 1.0), [], [cbuf])
        C.op("pool", lambda: G.affine_select(out=identF.ap[:], in_=onesF.ap[:], pattern=[[-1, 128]], compare_op=ALU.is_equal,
                                             fill=0.0, base=0, channel_multiplier=1), [cbuf], [identF.b])
        C.op("pool", lambda: G.affine_select(out=mle.ap[:], in_=onesF.ap[:], pattern=[[1, 128]], compare_op=ALU.is_ge,
                                             fill=0.0, base=0, channel_multiplier=-1), [cbuf], [mle.b])
        C.op("pool", lambda: G.affine_select(out=mgt.ap[:], in_=onesF.ap[:], pattern=[[-1, 128]], compare_op=ALU.is_gt,
                                             fill=0.0, base=0, channel_multiplier=1), [cbuf], [mgt.b])
        vcopy(identB.ap[:], identF.ap[:], [identF.b], [identB.b])
        onesB = mk("onesB", [128, 128], BF16)
        mgtB = mk("mgtB", [128, 128], BF16)
        vcopy(onesB.ap[:], onesF.ap[:], [cbuf], [onesB.b])
        vcopy(mgtB.ap[:], mgt.ap[:], [mgt.b], [mgtB.b])

        vt = mk("vt", [128, 256])
        VCOL = {}
        with ExitStack() as es0:
            st1 = mk("vst1", [128, 128], stack=es0)
            st2 = mk("vst2", [128, 128], stack=es0)
            C.op("dve", lambda: V.memset(st1.ap[:], 0.0), [], [st1.b])
            C.op("dve", lambda: V.memset(st2.ap[:], 0.0), [], [st2.b])
            rows1 = [("norm_w", D["norm_w"].rearrange("l (c p) -> (l c) p", p=128), 16),
                     ("norm_f", D["norm_f"].rearrange("(c p) -> c p", p=128), 8),
                     ("a_conv_w", D["a_conv_w"].rearrange("k (c p) -> (k c) p", p=128), 32),
                     ("a_conv_b", D["a_conv_b"].rearrange("(c p) -> c p", p=128), 8),
                     ("a_b_r", D["a_b_r"].rearrange("(c p) -> c p", p=128), 8),
                     ("a_b_i", D["a_b_i"].rearrange("(c p) -> c p", p=128), 8),
                     ("a_lam", D["a_lam"].rearrange("(c p) -> c p", p=128), 8),
                     ("b_norm_w", D["b_norm_w"].rearrange("(c p) -> c p", p=128), 8),
                     ("c_norm_w", D["c_norm_w"].rearrange("(c p) -> c p", p=128), 16)]
            rows2 = [("b_conv_w", D["b_conv_w"].rearrange("k (c p) -> (k c) p", p=128), 48),
                     ("b_conv_b", D["b_conv_b"].rearrange("(c p) -> c p", p=128), 12),
                     ("c_lb", D["c_lb"].rearrange("l (c p) -> (l c) p", p=128), 32)]
            for st, rows, base in ((st1, rows1, 0), (st2, rows2, 128)):
                r0 = 0
                for nm, ap, n in rows:
                    VCOL[nm] = base + r0
                    C.dma("sp", st.ap[r0:r0 + n, :], ap, reads=[], writes=[st.b])
                    r0 += n
                pb, pbb = bank()
                tp(pb[:, 0:128], st.ap[:, :], identF.ap[:], [st.b, identF.b], [pbb], True)
                vcopy(vt.ap[:, base:base + 128], pb[:, 0:128], [pbb], [vt.b])
            C.barrier(engines=("pe", "act", "dve", "pool", "sp"), with_pool=True)

        def vcol(nm, i):
            j = VCOL[nm] + i
            return vt.ap[:, j:j + 1]

        dv = mk("dv", [128, 64])
        la = VCOL["a_lam"]
        act(dv.ap[:, 0:8], vt.ap[:, la:la + 8], AF.Exp, [vt.b], [dv.b], scale=-1.0)
        act(dv.ap[:, 0:8], dv.ap[:, 0:8], AF.Ln, [dv.b], [dv.b], bias=1.0, scale=1.0)
        ts(dv.ap[:, 8:16], dv.ap[:, 0:8], -16.0, None, ALU.mult, None, [dv.b], [dv.b])
        ts(dv.ap[:, 0:8], dv.ap[:, 0:8], -8.0, None, ALU.mult, None, [dv.b], [dv.b])
        cl = VCOL["c_lb"]
        tt(dv.ap[:, 16:32], vt.ap[:, cl + 16:cl + 32], vt.ap[:, cl:cl + 16], ALU.subtract, [vt.b, dv.b], [dv.b])
        act(dv.ap[:, 16:32], dv.ap[:, 16:32], AF.Sigmoid, [dv.b], [dv.b])
        ts(dv.ap[:, 32:48], dv.ap[:, 16:32], -1.0, 1.0, ALU.mult, ALU.add, [dv.b], [dv.b])
        ts(dv.ap[:, 48:64], dv.ap[:, 32:48], -1.0, None, ALU.mult, None, [dv.b], [dv.b])

        hb = mk("hb", [128, 64])
        C.dma("sp", hb.ap[:, 0:16], D["b_dt_bias"].rearrange("(o h) -> o h", o=1).partition_broadcast(128), [], [hb.b])
        C.dma("sp", hb.ap[:, 16:32], D["b_a_log"].rearrange("(o h) -> o h", o=1).partition_broadcast(128), [], [hb.b])
        C.dma("sp", hb.ap[:, 32:48], D["b_d"].rearrange("(o h) -> o h", o=1).partition_broadcast(128), [], [hb.b])
        act(hb.ap[:, 16:32], hb.ap[:, 16:32], AF.Exp, [hb.b], [hb.b])
        ts(hb.ap[:, 16:32], hb.ap[:, 16:32], -1.0, None, ALU.mult, None, [hb.b], [hb.b])

        wdt = mk("wdt", [128, 8, 16], BF16)
        C.dma("pool", wdt.ap[:], D["ab_w_in"].rearrange("(k p) f -> p k f", p=128)[:, :, 4608:4624], [], [wdt.b])

        hT = mk("hT", [128, 8, NCOL], F32, nb=8)
        xnT = mk("xnT", [128, 8, NCOL], BF16, nb=8)
        hTb = xnT
        tailA = mk("tailA", [128, 8, 3], F32, nb=8)
        tailB = mk("tailB", [128, 12, 3], F32, nb=12)
        hlast = mk("hlast", [128, 8], F32)
        Sssm = mk("Sssm", [128, 1024], F32, nb=None)
        Sssm_b = [Buf("Sssm0"), Buf("Sssm1")]
        Sc = mk("Sc", [128, 16, 128], F32, nb=16)
        mixT = mk("mixT", [128, 8, NCOL], BF16, nb=8)
        C.op("dve", lambda: V.memset(Sssm.ap[:], 0.0), [], Sssm_b)
        C.op("dve", lambda: V.memset(Sc.ap[:], 0.0), [], Sc.b)

        def run_streams(gens, skew, max_active=2, filler=None, filler_rate=2):
            pending = list(gens)
            active = []
            steps = {}
            while pending or active or filler is not None:
                if pending and (not active or (len(active) < max_active and steps[id(active[-1])] >= skew)):
                    g = pending.pop(0)
                    active.append(g)
                    steps[id(g)] = 0
                for g in list(active):
                    try:
                        next(g)
                        steps[id(g)] += 1
                    except StopIteration:
                        assert filler is None, "filler stream must finish before any stream ends"
                        active.remove(g)
                if filler is not None:
                    for _ in range(filler_rate):
                        try:
                            next(filler)
                        except StopIteration:
                            filler = None
                            break

        def tiles_of(ncols):
            ts_ = [(0, 512), (512, 512)]
            if ncols > TH:
                ts_.append((TH, ncols - TH))
            return ts_

        def rmsnorm(ncols, wname, wbase, out_t, out_is_f32=False):
            tls = tiles_of(ncols)
            pbs = [bank() for _ in tls]
            with ExitStack() as esl:
                sq = [mk(f"sq{i}", [128, NCOL], BF16, stack=esl) for i in range(2)]
                rstd = mk("rstd", [128, NCOL], stack=esl)
                for c in range(8):
                    s = sq[c % 2]
                    act(s.ap[:, 0:ncols], hT.ap[:, c, 0:ncols], AF.Square, [hT.b[c]], [s.b])
                    for (t0, tn), (pb, pbb) in zip(tls, pbs):
                        mm(pb[:, 0:tn], onesB.ap[:], s.ap[:, t0:t0 + tn], c == 0, c == 7, [s.b, onesB.b], [pbb], c == 7)
                for (t0, tn), (pb, pbb) in zip(tls, pbs):
                    act(rstd.ap[:, t0:t0 + tn], pb[:, 0:tn], AF.Ln, [pbb], [rstd.b], scale=1.0 / 1024.0, bias=EPS)
                act(rstd.ap[:, 0:ncols], rstd.ap[:, 0:ncols], AF.Exp, [rstd.b], [rstd.b], scale=-0.5)
                for c in range(8):
                    stt(out_t.ap[:, c, 0:ncols], hT.ap[:, c, 0:ncols], vcol(wname, wbase + c), rstd.ap[:, 0:ncols],
                        ALU.mult, ALU.mult, [hT.b[c], rstd.b, vt.b], [out_t.b[c]])
                C.barrier()

        def proj_fm(wv, wb, fcol, src, src_b, ncols, evac, kcs=8):
            for (t0, tn) in tiles_of(ncols):
                pb, pbb = bank()
                for kc in range(kcs):
                    mm(pb[:, 0:tn], wv[:, kc, fcol:fcol + 128], src[:, kc, t0:t0 + tn], kc == 0, kc == kcs - 1,
                       [wb, src_b[kc]], [pbb], kc == kcs - 1)
                evac(pb, pbb, t0, tn)

        def out_proj_gen(Wrows, ncols):
            blocks = [wload(Wrows, 8, 0, 512), wload(Wrows, 8, 512, 512)]
            for c in range(8):
                wv, wb = blocks[c // 4]
                for (t0, tn) in tiles_of(ncols):
                    pb, pbb = bank()
                    for kc in range(8):
                        mm(pb[:, 0:tn], wv[:, kc, (c % 4) * 128:(c % 4 + 1) * 128], mixT.ap[:, kc, t0:t0 + tn], kc == 0, kc == 7,
                           [wb, mixT.b[kc]], [pbb], True)
                    tt(hT.ap[:, c, t0:t0 + tn], hT.ap[:, c, t0:t0 + tn], pb[:, 0:tn], ALU.add, [pbb, hT.b[c]], [hT.b[c]])
                    yield

        def out_proj(Wrows, ncols):
            blocks = [wload(Wrows, 8, 0, 512), wload(Wrows, 8, 512, 512)]
            for c in range(8):
                wv, wb = blocks[c // 4]

                def ev(pb, pbb, t0, tn, c=c):
                    tt(hT.ap[:, c, t0:t0 + tn], hT.ap[:, c, t0:t0 + tn], pb[:, 0:tn], ALU.add, [pbb, hT.b[c]], [hT.b[c]])
                proj_fm(wv, wb, (c % 4) * 128, mixT.ap, mixT.b, ncols, ev, kcs=8)

        def ple(layer, ps_i, ncols, pre=None):
            Wg = D["ple_gate"][layer]
            Wp = D["ple_proj"][layer]
            with ExitStack() as esl:
                pin = mk("pin", [128, 8, 256], stack=esl)
                psn = mk("psn", [16, 256], stack=esl)
                pTl = mk("pTl", [128, 2, NCOL], BF16, stack=esl)
                sg = [mk(f"plesg{i}", [128, 512], stack=esl) for i in range(2)]
                r0 = ps_i * TH
                C.dma("sp", pin.ap[:], D["pp"][layer, r0:r0 + TH, :].rearrange("(j p) f -> p j f", p=128), [], [pin.b])
                if ncols > TH:
                    C.dma("sp", psn.ap[:], D["psm"][layer], [], [psn.b])
                if pre is not None:
                    pre()
                gblk = [wload(Wg, 8, 0, 512)]
                pblk = wload(Wp, 2, 0, 1024)
                gblk.append(wload(Wg, 8, 512, 512))
                for kc in range(2):
                    for g in range(2):
                        pb, pbb = bank()
                        for j in range(4):
                            tp(pb[:, j * 128:(j + 1) * 128], pin.ap[:, g * 4 + j, kc * 128:(kc + 1) * 128], identF.ap[:],
                               [pin.b, identF.b], [pbb], j == 3)
                        (acopy if g == 0 else vcopy)(pTl.ap[:, kc, g * 512:(g + 1) * 512], pb[:, :], [pbb], [pTl.b])
                if ncols > TH:
                    for kc in range(2):
                        pb, pbb = bank()
                        tp(pb[:, 0:16], psn.ap[:, kc * 128:(kc + 1) * 128], identF.ap[0:16, 0:16], [psn.b, identF.b], [pbb], True)
                        vcopy(pTl.ap[:, kc, TH:TH + 16], pb[:, 0:16], [pbb], [pTl.b])
                for c in range(8):
                    (acopy if c % 2 == 0 else vcopy)(hTb.ap[:, c, 0:ncols], hT.ap[:, c, 0:ncols], [hT.b[c]], [hTb.b[c]])
                k = 0
                for c in range(8):
                    wv, wb = gblk[c // 4]
                    for (t0, tn) in tiles_of(ncols):
                        pg, pgb = bank()
                        for kc in range(8):
                            mm(pg[:, 0:tn], wv[:, kc, (c % 4) * 128:(c % 4 + 1) * 128], hTb.ap[:, kc, t0:t0 + tn], kc == 0, kc == 7,
                               [wb, hTb.b[kc]], [pgb], kc == 7)
                        pq, pqb = bank()
                        for kc in range(2):
                            mm(pq[:, 0:tn], pblk[0][:, kc, c * 128:(c + 1) * 128], pTl.ap[:, kc, t0:t0 + tn], kc == 0, kc == 1,
                               [pblk[1], pTl.b], [pqb], kc == 1)
                        s_ = sg[k % 2]
                        k += 1
                        act(s_.ap[:, 0:tn], pg[:, 0:tn], AF.Sigmoid, [pgb], [s_.b])
                        tt(s_.ap[:, 0:tn], s_.ap[:, 0:tn], pq[:, 0:tn], ALU.mult, [s_.b, pqb], [s_.b])
                        tt(hT.ap[:, c, t0:t0 + tn], hT.ap[:, c, t0:t0 + tn], s_.ap[:, 0:tn], ALU.add, [s_.b, hT.b[c]], [hT.b[c]])
                C.barrier()

        def phase0(ps_i, ncols):
            with ExitStack() as esl:
                xin = [mk(f"xin{i}", [128, 4, 1024], stack=esl) for i in range(2)]
                xsn = mk("xsn", [16, 1024], stack=esl)
                k = 0
                for g in range(2):
                    xi = xin[g]
                    r0 = ps_i * TH + g * 512
                    C.dma("sp", xi.ap[:], D["xp"][r0:r0 + 512, :].rearrange("(j p) f -> p j f", p=128), [], [xi.b])
                    for c in range(8):
                        pb, pbb = bank()
                        for j in range(4):
                            tp(pb[:, j * 128:(j + 1) * 128], xi.ap[:, j, c * 128:(c + 1) * 128], identF.ap[:], [xi.b, identF.b], [pbb], j == 3)
                        if k % 2 == 0:
                            acopy(hT.ap[:, c, g * 512:(g + 1) * 512], pb[:, :], [pbb], [hT.b[c]])
                        else:
                            vcopy(hT.ap[:, c, g * 512:(g + 1) * 512], pb[:, :], [pbb], [hT.b[c]])
                        k += 1
                if ncols > TH:
                    C.dma("sp", xsn.ap[:], D["xs"][:, :], [], [xsn.b])
                    for c in range(8):
                        pb, pbb = bank()
                        tp(pb[:, 0:16], xsn.ap[:, c * 128:(c + 1) * 128], identF.ap[0:16, 0:16], [xsn.b, identF.b], [pbb], True)
                        vcopy(hT.ap[:, c, TH:TH + 16], pb[:, 0:16], [pbb], [hT.b[c]])
                C.barrier()

        def layer0_A(ps_i, ncols):
            Win = D["ab_w_in"]
            has_s = ncols > TH
            n = ncols
            with ExitStack() as esl:
                sets = [dict(ax=mk("ax", [128, 3 + TH], BF16, stack=esl), dg=mk("dgA", [128, 4, 128], BF16, stack=esl),
                             xc=mk("xc", [128, NCOL], stack=esl), rr=mk("rr", [128, NCOL], stack=esl),
                             gi=mk("gi", [128, NCOL], stack=esl), aa=mk("aa", [128, NCOL], stack=esl),
                             hh=mk("hh", [128, NCOL], stack=esl)) for _ in range(NS_A)]
                tw = wpool[wstate["i"] % NW]
                wstate["i"] += 1
                wri = tw.ap[:, 0:2048].rearrange("p (k f) -> p k f", k=16)
                C.dma("pool", wri[:, 0:8, :], D["a_w_r"].rearrange("k c d -> c k d"), [], [tw.b])
                C.dma("pool", wri[:, 8:16, :], D["a_w_i"].rearrange("k c d -> c k d"), [], [tw.b])
                blk_x = [wload(Win, 8, 0, 512)]
                blk_g = [wload(Win, 8, 1024, 512)]
                blk_x.append(wload(Win, 8, 512, 512))
                blk_g.append(wload(Win, 8, 1536, 512))

                def chunk_gen(c, Bs):
                    ax, xc, rr, gi, aa, hh, dg = Bs["ax"], Bs["xc"], Bs["rr"], Bs["gi"], Bs["aa"], Bs["hh"], Bs["dg"]
                    xcb_ap = hh.ap[:, 0:NCOL // 2].bitcast(BF16)
                    wv, wb = blk_x[c // 4]
                    if ps_i == 0:
                        C.op("dve", lambda: V.memset(ax.ap[:, 0:3], 0.0), [], [ax.b])
                    else:
                        vcopy(ax.ap[:, 0:3], tailA.ap[:, c, :], [tailA.b[c]], [ax.b])
                    for kk in range(4):
                        ts(dg.ap[:, kk, :], identB.ap[:], vcol("a_conv_w", kk * 8 + c), None, ALU.mult, None, [identB.b, vt.b], [dg.b])

                    def ev_x(pb, pbb, t0, tn):
                        if t0 < TH:
                            vcopy(ax.ap[:, 3 + t0:3 + t0 + tn], pb[:, 0:tn], [pbb], [ax.b])
                            if t0 + tn == TH:
                                vcopy(tailA.ap[:, c, :], pb[:, tn - 3:tn], [pbb], [tailA.b[c]])
                        else:
                            vcopy(axsT.ap[:, c, :], pb[:, 0:tn], [pbb], [axsT.b])
                    proj_fm(wv, wb, (c % 4) * 128, xnT.ap, xnT.b, ncols, ev_x)
                    yield
                    for (t0, tn) in tiles_of(TH):
                        pb, pbb = bank()
                        for kk in range(4):
                            mm(pb[:, 0:tn], dg.ap[:, kk, :], ax.ap[:, t0 + kk:t0 + kk + tn], kk == 0, kk == 3, [dg.b, ax.b], [pbb], kk == 3)
                        ts(xc.ap[:, t0:t0 + tn], pb[:, 0:tn], vcol("a_conv_b", c), None, ALU.add, None, [pbb, vt.b], [xc.b])
                    yield
                    if has_s:
                        sl = slice(TH, TH + NS)
                        ts(xc.ap[:, sl], axsT.ap[:, c, :], vcol("a_conv_w", 3 * 8 + c), vcol("a_conv_b", c), ALU.mult, ALU.add,
                           [axsT.b, vt.b], [xc.b])
                        for kk in range(3):
                            stt(xc.ap[:, sl], sacT.ap[:, c, kk, :], vcol("a_conv_w", kk * 8 + c), xc.ap[:, sl], ALU.mult, ALU.add,
                                [sacT.b, xc.b, vt.b], [xc.b])
                    vcopy(xcb_ap[:, 0:ncols], xc.ap[:, 0:ncols], [xc.b], [hh.b])
                    yield
                    for (t0, tn) in tiles_of(ncols):
                        pb, pbb = bank()
                        mm(pb[:, 0:tn], wri[:, c, :], xcb_ap[:, t0:t0 + tn], True, True, [tw.b, hh.b], [pbb], True)
                        act(rr.ap[:, t0:t0 + tn], pb[:, 0:tn], AF.Sigmoid, [pbb, vt.b], [rr.b], bias=vcol("a_b_r", c), scale=1.0)
                        pb, pbb = bank()
                        mm(pb[:, 0:tn], wri[:, 8 + c, :], xcb_ap[:, t0:t0 + tn], True, True, [tw.b, hh.b], [pbb], True)
                        act(gi.ap[:, t0:t0 + tn], pb[:, 0:tn], AF.Sigmoid, [pbb, vt.b], [gi.b], bias=vcol("a_b_i", c), scale=1.0)
                    yield
                    act(aa.ap[:, 0:n], rr.ap[:, 0:n], AF.Exp, [rr.b, dv.b], [aa.b], scale=dv.ap[:, c:c + 1])
                    act(rr.ap[:, 0:n], rr.ap[:, 0:n], AF.Exp, [rr.b, dv.b], [rr.b], scale=dv.ap[:, 8 + c:9 + c])
                    yield
                    act(rr.ap[:, 0:n], rr.ap[:, 0:n], AF.Sqrt, [rr.b], [rr.b], scale=-1.0, bias=1.0)
                    tt(gi.ap[:, 0:n], gi.ap[:, 0:n], rr.ap[:, 0:n], ALU.mult, [gi.b, rr.b], [gi.b])
                    tt(gi.ap[:, 0:n], gi.ap[:, 0:n], xc.ap[:, 0:n], ALU.mult, [gi.b, xc.b], [gi.b])
                    yield
                    init = 0.0 if ps_i == 0 else hlast.ap[:, c:c + 1]
                    C.op("dve", lambda: V.tensor_tensor_scan(out=hh.ap[:, 0:TH], data0=aa.ap[:, 0:TH], data1=gi.ap[:, 0:TH], initial=init,
                                                             op0=ALU.mult, op1=ALU.add), [aa.b, gi.b, hlast.b], [hh.b])
                    vcopy(hlast.ap[:, c:c + 1], hh.ap[:, TH - 1:TH], [hh.b], [hlast.b])
                    if has_s:
                        sl = slice(TH, TH + NS)
                        tt(hh.ap[:, sl], aa.ap[:, sl], sahT.ap[:, c, :], ALU.mult, [aa.b, sahT.b], [hh.b])
                        tt(hh.ap[:, sl], hh.ap[:, sl], gi.ap[:, sl], ALU.add, [hh.b, gi.b], [hh.b])
                        vcopy(ahsT.ap[:, c, :], hh.ap[:, sl], [hh.b], [ahsT.b])
                    yield
                    wv, wb = blk_g[c // 4]

                    def ev_g(pb, pbb, t0, tn):
                        act(rr.ap[:, t0:t0 + tn], pb[:, 0:tn], AF.Silu, [pbb], [rr.b])
                        tt(mixT.ap[:, c, t0:t0 + tn], hh.ap[:, t0:t0 + tn], rr.ap[:, t0:t0 + tn], ALU.mult, [hh.b, rr.b], [mixT.b[c]])
                    proj_fm(wv, wb, (c % 4) * 128, xnT.ap, xnT.b, ncols, ev_g)
                    yield

                run_streams([chunk_gen(c, sets[c % NS_A]) for c in range(8)], skew=SKEW_A, max_active=NS_A)
                C.barrier()

        def layer0_B(ps_i, ncols):
            Win = D["ab_w_in"]
            has_s = ncols > TH
            NCH = TH // 128
            with ExitStack() as esl:
                csets = [dict(bx=mk("bx", [128, 3 + TH], BF16, stack=esl), bcs=mk("bcs", [128, NS], stack=esl),
                              dg=mk("dgB", [128, 4, 128], BF16, stack=esl)) for _ in range(2)]
                xbcT = mk("xbcT", [128, 6, NCOL], BF16, nb=6, stack=esl)
                dtt = mk("dtt", [128, NCH, 16], stack=esl)
                dta = mk("dta", [128, NCH, 16], stack=esl)
                Sb = mk("Sb", [128, 512], BF16, stack=esl)
                ssets = [dict(eall=mk("eall", [128, 3, 16], stack=esl), Rm=mk("Rm", [128, 8, 128], BF16, stack=esl),
                              Ee=mk("Ee", [128, 8, 128], BF16, stack=esl), cbm=mk("cbm", [128, 128], stack=esl),
                              Wt=mk("Wt", [128, 8, 128], BF16, stack=esl), xtm=mk("xtm", [128, 512], BF16, stack=esl),
                              btm=mk("btm", [128, 128], BF16, stack=esl), xdt=mk("xdt", [128, 512], BF16, stack=esl),
                              xD=mk("xD", [128, 512], BF16, stack=esl), xdt2=mk("xdt2", [128, 512], BF16, stack=esl),
                              ytmp=mk("ytmp", [128, 512], stack=esl), yy=mk("yy", [128, 512], stack=esl),
                              sz=mk("sz", [128, 512], stack=esl), ynb=mk("ynb", [128, 512], BF16, stack=esl),
                              ssq=mk("ssq", [128, 2], stack=esl)) for _ in range(NS_S)]

                pb, pbb = bank()
                for j in range(NCH):
                    for kc in range(8):
                        mm(pb[:, j * 16:(j + 1) * 16], xnT.ap[:, kc, j * 128:(j + 1) * 128], wdt.ap[:, kc, :], kc == 0, kc == 7,
                           [xnT.b[kc], wdt.b], [pbb], kc == 7 and j == NCH - 1)
                tt(dtt.ap[:], pb[:, 0:NCH * 16].rearrange("p (j h) -> p j h", h=16),
                   hb.ap[:, 0:16].unsqueeze(1).to_broadcast([128, NCH, 16]), ALU.add, [pbb, hb.b], [dtt.b])
                act(dtt.ap[:], dtt.ap[:], AF.Exp, [dtt.b], [dtt.b])
                act(dtt.ap[:], dtt.ap[:], AF.Ln, [dtt.b], [dtt.b], bias=1.0, scale=1.0)
                tt(dta.ap[:], dtt.ap[:], hb.ap[:, 16:32].unsqueeze(1).to_broadcast([128, NCH, 16]), ALU.mult, [dtt.b, hb.b], [dta.b])

                for g in range(2):
                    wx = wload(Win, 8, 3072 + 512 * g, 512)
                    wbcblk = wload(Win, 8, 4096, 512)
                    zblk = wload(Win, 8, 2048 + 512 * g, 512)

                    def conv_gen(i, Bs, g=g, wx=wx, wbcblk=wbcblk):
                        bx, bcs, dg = Bs["bx"], Bs["bcs"], Bs["dg"]
                        if i < 4:
                            wv, wb, fcol, cidx = wx[0], wx[1], i * 128, 4 * g + i
                        elif i == 4:
                            wv, wb, fcol, cidx = wbcblk[0], wbcblk[1], 128 * g, 8 + g
                        else:
                            wv, wb, fcol, cidx = wbcblk[0], wbcblk[1], 256 + 128 * g, 10 + g
                        if ps_i == 0:
                            C.op("dve", lambda: V.memset(bx.ap[:, 0:3], 0.0), [], [bx.b])
                        else:
                            vcopy(bx.ap[:, 0:3], tailB.ap[:, cidx, :], [tailB.b[cidx]], [bx.b])
                        for kk in range(4):
                            ts(dg.ap[:, kk, :], identB.ap[:], vcol("b_conv_w", kk * 12 + cidx), None, ALU.mult, None, [identB.b, vt.b], [dg.b])

                        def ev_x(pb, pbb, t0, tn):
                            if t0 < TH:
                                acopy(bx.ap[:, 3 + t0:3 + t0 + tn], pb[:, 0:tn], [pbb], [bx.b])
                                if t0 + tn == TH:
                                    acopy(tailB.ap[:, cidx, :], pb[:, tn - 3:tn], [pbb], [tailB.b[cidx]])
                            else:
                                acopy(bxsT.ap[:, cidx, :], pb[:, 0:tn], [pbb], [bxsT.b])
                        proj_fm(wv, wb, fcol, xnT.ap, xnT.b, ncols, ev_x)
                        yield
                        for (t0, tn) in tiles_of(TH):
                            pb, pbb = bank()
                            for kk in range(4):
                                mm(pb[:, 0:tn], dg.ap[:, kk, :], bx.ap[:, t0 + kk:t0 + kk + tn], kk == 0, kk == 3, [dg.b, bx.b], [pbb], kk == 3)
                            act(xbcT.ap[:, i, t0:t0 + tn], pb[:, 0:tn], AF.Silu, [pbb, vt.b], [xbcT.b[i]], bias=vcol("b_conv_b", cidx), scale=1.0)
                        yield
                        if has_s:
                            ts(bcs.ap[:], bxsT.ap[:, cidx, :], vcol("b_conv_w", 3 * 12 + cidx), vcol("b_conv_b", cidx),
                               ALU.mult, ALU.add, [bxsT.b, vt.b], [bcs.b])
                            for kk in range(3):
                                stt(bcs.ap[:], sbcT.ap[:, cidx, kk, :], vcol("b_conv_w", kk * 12 + cidx), bcs.ap[:], ALU.mult, ALU.add,
                                    [sbcT.b, bcs.b, vt.b], [bcs.b])
                            act(xbcT.ap[:, i, TH:TH + NS], bcs.ap[:], AF.Silu, [bcs.b], [xbcT.b[i]])
                            act(xbcsT.ap[:, cidx, :], bcs.ap[:], AF.Silu, [bcs.b], [xbcsT.b])
                            yield

                    run_streams([conv_gen(i, csets[i % 2]) for i in range(6)], skew=SKEW_C)

                    Sg = Sssm.ap[:, 512 * g:512 * (g + 1)]
                    Sgb = Sssm_b[g]
                    acopy(Sb.ap[:], Sg, [Sgb], [Sb.b])

                    def ssd_gen(j, Bs, g=g, Sg=Sg, Sgb=Sgb, zblk=zblk):
                        eall, Rm, Ee, cbm, Wt, xtm, btm = Bs["eall"], Bs["Rm"], Bs["Ee"], Bs["cbm"], Bs["Wt"], Bs["xtm"], Bs["btm"]
                        xdt, xD, xdt2, ytmp, yy, sz, ynb, ssq, stmp = (Bs["xdt"], Bs["xD"], Bs["xdt2"], Bs["ytmp"], Bs["yy"], Bs["sz"],
                                                                       Bs["ynb"], Bs["ssq"], Bs["ytmp"])
                        cs = slice(j * 128, (j + 1) * 128)
                        hs = slice(8 * g, 8 * g + 8)
                        pbt, pbtb = bank()
                        ptb = pbt[:, :].bitcast(BF16)
                        for i in range(4):
                            tp(ptb[:, i * 128:(i + 1) * 128], xbcT.ap[:, i, cs], identB.ap[:], [xbcT.b[i], identB.b], [pbtb], False)
                        tp(ptb[:, 512:640], xbcT.ap[:, 4, cs], identB.ap[:], [xbcT.b[4], identB.b], [pbtb], True)
                        acopy(xtm.ap[:], ptb[:, 0:512], [pbtb], [xtm.b])
                        acopy(btm.ap[:], ptb[:, 512:640], [pbtb], [btm.b])
                        yield
                        pd, pdb = bank()
                        mm(pd[:, 0:8], mle.ap[:], dta.ap[:, j, hs], True, True, [mle.b, dta.b], [pdb], False)
                        mm(pd[:, 16:24], mgt.ap[:], dta.ap[:, j, hs], True, True, [mgt.b, dta.b], [pdb], False)
                        mm(pd[:, 32:40], onesF.ap[:], dta.ap[:, j, hs], True, True, [cbuf, dta.b], [pdb], True)
                        act(eall.ap[:, :, 0:8], pd[:, 0:48].rearrange("p (a h) -> p a h", h=16)[:, :, 0:8], AF.Exp, [pdb], [eall.b])
                        tt(Rm.ap[:], mle.ap[:].unsqueeze(1).to_broadcast([128, 8, 128]),
                           dta.ap[:, j, hs].unsqueeze(2).to_broadcast([128, 8, 128]), ALU.mult, [mle.b, dta.b], [Rm.b])
                        yield
                        for q in range(2):
                            pD, pDb = bank()
                            mm(pD[:, :], mgtB.ap[:], Rm.ap[:, 4 * q:4 * q + 4, :].rearrange("p h t -> p (h t)"), True, True,
                               [mgtB.b, Rm.b], [pDb], True)
                            act(Ee.ap[:, 4 * q:4 * q + 4, :].rearrange("p h t -> p (h t)"), pD[:, :], AF.Exp, [pDb], [Ee.b])
                        yield
                        pc, pcb = bank()
                        mm(pc[:, 0:128], xbcT.ap[:, 4, cs], xbcT.ap[:, 5, cs], True, True, [xbcT.b[4], xbcT.b[5]], [pcb], True)
                        tt(cbm.ap[:], pc[:, 0:128], mle.ap[:], ALU.mult, [pcb, mle.b], [cbm.b])
                        tt(Wt.ap[:], Ee.ap[:], cbm.ap[:].unsqueeze(1).to_broadcast([128, 8, 128]), ALU.mult, [Ee.b, cbm.b], [Wt.b])
                        yield
                        x3 = xtm.ap[:].rearrange("p (h q) -> p h q", q=64)
                        tt(xdt.ap[:].rearrange("p (h q) -> p h q", q=64), x3, dtt.ap[:, j, hs].unsqueeze(2).to_broadcast([128, 8, 64]),
                           ALU.mult, [xtm.b, dtt.b], [xdt.b])
                        tt(xD.ap[:].rearrange("p (h q) -> p h q", q=64), x3, hb.ap[:, 32 + 8 * g:40 + 8 * g].unsqueeze(2).to_broadcast([128, 8, 64]),
                           ALU.mult, [xtm.b, hb.b], [xD.b])
                        tt(xdt2.ap[:].rearrange("p (h q) -> p h q", q=64), xdt.ap[:].rearrange("p (h q) -> p h q", q=64),
                           eall.ap[:, 1, 0:8].unsqueeze(2).to_broadcast([128, 8, 64]), ALU.mult, [xdt.b, eall.b], [xdt2.b])
                        yield
                        py, pyb = bank()
                        mm(py[:, :], identB.ap[:], xD.ap[:], True, False, [identB.b, xD.b], [pyb], False)
                        for h in range(8):
                            mm(py[:, h * 64:(h + 1) * 64], Wt.ap[:, h, :], xdt.ap[:, h * 64:(h + 1) * 64], False, h == 7,
                               [Wt.b, xdt.b], [pyb], h == 7)
                        po, pob = bank()
                        mm(po[:, :], xbcT.ap[:, 5, cs], Sb.ap[:], True, True, [xbcT.b[5], Sb.b], [pob], True)
                        tt(ytmp.ap[:].rearrange("p (h q) -> p h q", q=64), po[:, :].rearrange("p (h q) -> p h q", q=64),
                           eall.ap[:, 0, 0:8].unsqueeze(2).to_broadcast([128, 8, 64]), ALU.mult, [pob, eall.b], [ytmp.b])
                        tt(yy.ap[:], ytmp.ap[:], py[:, :], ALU.add, [ytmp.b, pyb], [yy.b])
                        yield
                        pst, pstb = bank()
                        mm(pst[:, :], btm.ap[:], xdt2.ap[:], True, True, [btm.b, xdt2.b], [pstb], True)
                        tt(stmp.ap[:].rearrange("p (h q) -> p h q", q=64), Sg.rearrange("p (h q) -> p h q", q=64),
                           eall.ap[:, 2, 0:8].unsqueeze(2).to_broadcast([128, 8, 64]), ALU.mult, [Sgb, eall.b], [stmp.b])
                        tt(Sg, stmp.ap[:], pst[:, :], ALU.add, [stmp.b, pstb], [Sgb])
                        acopy(Sb.ap[:], Sg, [Sgb], [Sb.b])
                        yield
                        for _ in gate_norm_gen(g, xnT.ap[:, :, cs], 128, yy, zblk, sz, ssq, ynb, j * 128):
                            yield

                    fill = out_proj_gen(D["ab_w_out"][0:1024, :], ncols) if g == 0 else None
                    run_streams([ssd_gen(j, ssets[j % NS_S]) for j in range(NCH)], skew=SKEW_S, max_active=NS_S, filler=fill, filler_rate=3)
                C.barrier()

        def gate_norm_gen(g, xsrc, M, yy, zb, sz, ssq, ynb, col0):
            pz, pzb = bank()
            for kc in range(8):
                mm(pz[0:M, :], xsrc[:, kc, :], zb[0][:, kc, :], kc == 0, kc == 7, [xnT.b[kc], zb[1]], [pzb], kc == 7)
            act(sz.ap[0:M, :], pz[0:M, :], AF.Silu, [pzb], [sz.b])
            yield
            tt(yy.ap[0:M, :], yy.ap[0:M, :], sz.ap[0:M, :], ALU.mult, [yy.b, sz.b], [yy.b])
            act(sz.ap[0:M, :], yy.ap[0:M, :], AF.Square, [yy.b], [sz.b])
            C.op("dve", lambda: V.tensor_reduce(out=ssq.ap[0:M, 0:1], in_=sz.ap[0:M, :], axis=AX.X, op=ALU.add), [sz.b], [ssq.b])
            yield
            act(ssq.ap[0:M, 1:2], ssq.ap[0:M, 0:1], AF.Ln, [ssq.b], [ssq.b], scale=1.0 / 512.0, bias=EPS)
            act(ssq.ap[0:M, 1:2], ssq.ap[0:M, 1:2], AF.Exp, [ssq.b], [ssq.b], scale=-0.5)
            ts(ynb.ap[0:M, :], yy.ap[0:M, :], ssq.ap[0:M, 1:2], None, ALU.mult, None, [yy.b, ssq.b], [ynb.b])
            yield
            pt_, ptb_ = bank()
            pv = pt_[:, :].bitcast(BF16)
            for i in range(4):
                tp(pv[:, i * 128:i * 128 + M], ynb.ap[0:M, i * 128:(i + 1) * 128], identB.ap[0:M, 0:M], [ynb.b, identB.b], [ptb_], i == 3)
            for i in range(4):
                ci = 4 * g + i
                act(mixT.ap[:, ci, col0:col0 + M], pv[:, i * 128:i * 128 + M], AF.Copy, [ptb_, vt.b], [mixT.b[ci]],
                    scale=vcol("b_norm_w", ci))
            yield

        def gate_norm_out(g, xsrc, M, yy, zb, sz, ssq, ynb, col0):
            for _ in gate_norm_gen(g, xsrc, M, yy, zb, sz, ssq, ynb, col0):
                pass

        axsT = mk("axsT", [128, 8, NS])
        bxsT = mk("bxsT", [128, 12, NS])
        ahsT = mk("ahsT", [128, 8, NS])

        def load_sample_states():
            with ExitStack() as esl:
                s1 = mk("ss1", [16, 3 * 1024], stack=esl)
                s2 = mk("ss2", [16, 3 * 1536], stack=esl)
                s3 = mk("ss3", [16, 1024], stack=esl)
                C.dma("sp", s1.ap[:], D["sac"].rearrange("b k f -> b (k f)"), [], [s1.b])
                C.dma("sp", s2.ap[:], D["sbc"].rearrange("b k f -> b (k f)"), [], [s2.b])
                C.dma("sp", s3.ap[:], D["sah"][:, :], [], [s3.b])
                idn = identF.ap[0:16, 0:16]
                for (src, dst, nchunk, nk, width) in ((s1, sacT, 8, 3, 1024), (s2, sbcT, 12, 3, 1536)):
                    for kk in range(nk):
                        pb, pbb = bank()
                        for c in range(nchunk):
                            tp(pb[:, c * 16:(c + 1) * 16], src.ap[:, kk * width + c * 128:kk * width + (c + 1) * 128], idn,
                               [src.b, identF.b], [pbb], c == nchunk - 1)
                        vcopy(dst.ap[:, :, kk, :], pb[:, 0:nchunk * 16].rearrange("p (c b) -> p c b", b=16), [pbb], [dst.b])
                pb, pbb = bank()
                for c in range(8):
                    tp(pb[:, c * 16:(c + 1) * 16], s3.ap[:, c * 128:(c + 1) * 128], idn, [s3.b, identF.b], [pbb], c == 7)
                vcopy(sahT.ap[:], pb[:, 0:128].rearrange("p (c b) -> p c b", b=16), [pbb], [sahT.b])
                C.dma("sp", O["acs"][:, 0:2, :], D["sac"][:, 1:3, :], [], [])
                C.dma("sp", O["bcs"][:, 0:2, :], D["sbc"][:, 1:3, :], [], [])
                C.barrier()

        def fm_to_rows(src_t, nchunk, ncol, dst_ap, dst_b=None):
            with ExitStack() as esl:
                st = mk("f2r", [16, 1536], stack=esl)
                for c0 in range(0, nchunk, 4):
                    pb, pbb = bank()
                    n = min(4, nchunk - c0)
                    for c in range(n):
                        tp(pb[0:ncol, c * 128:(c + 1) * 128], src_t.ap[:, c0 + c, 0:ncol], identF.ap[:], [src_t.b if not isinstance(src_t.b, list) else src_t.b[c0 + c], identF.b],
                           [pbb], c == n - 1)
                    vcopy(st.ap[0:ncol, c0 * 128:(c0 + n) * 128], pb[0:ncol, 0:n * 128], [pbb], [st.b])
                C.dma("sp", dst_ap, st.ap[0:ncol, 0:nchunk * 128], [st.b], [] if dst_b is None else [dst_b])
                C.barrier()

        def ssd_samples(g):
            with ExitStack() as esl:
                zb = wload(D["ab_w_in"], 8, 2048 + 512 * g, 512)
                yy = mk("s_yy", [16, 512], stack=esl)
                sz = mk("s_sz", [16, 512], stack=esl)
                ynb = mk("s_ynb", [16, 512], BF16, stack=esl)
                ssq = mk("s_ssq", [16, 2], stack=esl)
                tm = mk("s_tm", [16, 1024], stack=esl)
                dts = mk("s_dt", [16, 16], stack=esl)
                dec = mk("s_dec", [16, 16], stack=esl)
                xdtm = mk("s_xdt", [16, 512], BF16, stack=esl)
                decx = mk("s_decx", [16, 512], stack=esl)
                decT = mk("s_decT", [128, 4, 16], stack=esl)
                bcb = mk("s_bcb", [128, 16, 128], stack=esl)
                lms = [mk("s_lm", [16, 16, 128], BF16, stack=esl) for _ in range(2)]
                btmb = mk("s_btm", [16, 128], BF16, stack=esl)
                S0 = [mk(f"s_S0{i}", [128, 16, 128], stack=esl) for i in range(2)]
                t3 = mk("s_t3", [128, 16, 128], stack=esl)
                yT = mk("s_yT", [128, 4, 16], stack=esl)
                oh = mk("s_oh", [16, 16], stack=esl)
                vcopy(oh.ap[:], identF.ap[0:16, 0:16], [identF.b], [oh.b])
                pb, pbb = bank()
                for i in range(4):
                    tp(pb[0:16, i * 128:(i + 1) * 128], xbcsT.ap[:, 4 * g + i, :], identF.ap[:], [xbcsT.b, identF.b], [pbb], False)
                pb2, pbb2 = bank()
                tp(pb2[0:16, 0:128], xbcsT.ap[:, 8 + g, :], identF.ap[:], [xbcsT.b, identF.b], [pbb2], False)
                tp(pb2[0:16, 128:256], xbcsT.ap[:, 10 + g, :], identF.ap[:], [xbcsT.b, identF.b], [pbb2], True)
                vcopy(tm.ap[:, 0:512], pb[0:16, :], [pbb, pbb2], [tm.b])
                vcopy(tm.ap[:, 512:768], pb2[0:16, 0:256], [pbb2], [tm.b])
                pd, pdb = bank()
                for kc in range(8):
                    mm(pd[0:16, 0:16], xnT.ap[:, kc, TH:TH + NS], wdt.ap[:, kc, :], kc == 0, kc == 7, [xnT.b[kc], wdt.b], [pdb], kc == 7)
                tt(dts.ap[:], pd[0:16, 0:16], hb.ap[0:16, 0:16], ALU.add, [pdb, hb.b], [dts.b])
                act(dts.ap[:], dts.ap[:], AF.Exp, [dts.b], [dts.b])
                act(dts.ap[:], dts.ap[:], AF.Ln, [dts.b], [dts.b], bias=1.0, scale=1.0)
                tt(dec.ap[:], dts.ap[:], hb.ap[0:16, 16:32], ALU.mult, [dts.b, hb.b], [dec.b])
                act(dec.ap[:], dec.ap[:], AF.Exp, [dec.b], [dec.b])
                hs = slice(8 * g, 8 * g + 8)
                tt(xdtm.ap[:].rearrange("p (h q) -> p h q", q=64), tm.ap[:, 0:512].rearrange("p (h q) -> p h q", q=64),
                   dts.ap[:, hs].unsqueeze(2).to_broadcast([16, 8, 64]), ALU.mult, [tm.b, dts.b], [xdtm.b])
                vcopy(decx.ap[:].rearrange("p (h q) -> p h q", q=64), dec.ap[:, hs].unsqueeze(2).to_broadcast([16, 8, 64]), [dec.b], [decx.b])
                pb, pbb = bank()
                for i in range(4):
                    tp(pb[:, i * 16:(i + 1) * 16], decx.ap[:, i * 128:(i + 1) * 128], identF.ap[0:16, 0:16], [decx.b, identF.b], [pbb], i == 3)
                vcopy(decT.ap[:], pb[:, 0:64].rearrange("p (i b) -> p i b", b=16), [pbb], [decT.b])
                vcopy(btmb.ap[:], tm.ap[:, 512:640], [tm.b], [btmb.b])
                C.dma("sp", scr_bc[g].rearrange("(b n) -> b n", n=128), tm.ap[:, 640:768], [tm.b], [scr_bc_b])
                C.dma("sp", bcb.ap[:].rearrange("p b n -> p (b n)"), scr_bc[g:g + 1, :].partition_broadcast(128), [scr_bc_b], [bcb.b])
                def ld(i):
                    hp_ = 4 * g + i
                    C.dma("sp", S0[i % 2].ap[:], D["sbs"][:, 2 * hp_:2 * hp_ + 2, :, :].rearrange("b h q n -> (h q) b n"), [], [S0[i % 2].b])
                ld(0)
                for i in range(4):
                    hp = 4 * g + i
                    s0 = S0[i % 2]
                    lm = lms[i % 2]
                    if i + 1 < 4:
                        ld(i + 1)
                    tt(lm.ap[:], xdtm.ap[:, i * 128:(i + 1) * 128].unsqueeze(1).to_broadcast([16, 16, 128]),
                       oh.ap[:].unsqueeze(2).to_broadcast([16, 16, 128]), ALU.mult, [xdtm.b, oh.b], [lm.b])
                    pqs = [bank() for _ in range(4)]
                    for b in range(NS):
                        pq, pqb = pqs[b // 4]
                        mm(pq[:, (b % 4) * 128:(b % 4 + 1) * 128], lm.ap[:, b, :], btmb.ap[:], True, True, [lm.b, btmb.b], [pqb], True)
                    for b in range(NS):
                        act(s0.ap[:, b, :], s0.ap[:, b, :], AF.Copy, [s0.b, decT.b], [s0.b], scale=decT.ap[:, i, b:b + 1])
                    for q4 in range(4):
                        pq, pqb = pqs[q4]
                        tt(s0.ap[:, 4 * q4:4 * q4 + 4, :], s0.ap[:, 4 * q4:4 * q4 + 4, :], pq[:, :].rearrange("p (b v) -> p b v", v=128), ALU.add,
                           [s0.b, pqb], [s0.b])
                    C.dma("sp", O["bss"][:, 2 * hp:2 * hp + 2, :, :].rearrange("b h q n -> (h q) b n"), s0.ap[:], [s0.b], [])
                    tt(t3.ap[:], s0.ap[:], bcb.ap[:], ALU.mult, [s0.b, bcb.b], [t3.b])
                    C.op("dve", lambda: V.tensor_reduce(out=yT.ap[:, i, :], in_=t3.ap[:], axis=AX.X, op=ALU.add), [t3.b], [yT.b])
                pb, pbb = bank()
                for i in range(4):
                    tp(pb[0:16, i * 128:(i + 1) * 128], yT.ap[:, i, :], identF.ap[:], [yT.b, identF.b], [pbb], i == 3)
                tt(tm.ap[:, 0:512].rearrange("p (h q) -> p h q", q=64), tm.ap[:, 0:512].rearrange("p (h q) -> p h q", q=64),
                   hb.ap[0:16, 32 + 8 * g:40 + 8 * g].unsqueeze(2).to_broadcast([16, 8, 64]), ALU.mult, [tm.b, hb.b], [tm.b])
                tt(yy.ap[0:16, :], tm.ap[:, 0:512], pb[0:16, :], ALU.add, [tm.b, pbb], [yy.b])
                gate_norm_out(g, xnT.ap[:, :, TH:TH + NS], NS, yy, zb, sz, ssq, ynb, TH)
                C.barrier()

        def layer1(ps_i, ncols):
            Win = D["c_w_in"]
            has_s = ncols > TH
            NCH = TH // 128
            n = ncols
            with ExitStack() as esl:
                vtm = mk("vtm", [128, NCH, 512], BF16, stack=esl)
                vs = mk("c_vs", [16, 512], BF16, stack=esl)
                hsb = hgrn_alloc(esl) if has_s else None
                HC = 4
                sets = []
                nsets = NS_H if not has_s else 2
                for i in range(nsets):
                    sets.append(dict(
                        sg=mk("c_sg", [128, NCOL], stack=esl), gg=mk("c_g", [128, NCOL], stack=esl), Bc=mk("c_B", [128, NCOL], stack=esl),
                        qsc=mk("c_qsc", [128, NS], stack=esl), qraw=mk("c_qraw", [128, NCOL], BF16, stack=esl),
                        sgate=mk("c_sgate", [128, NCOL], BF16, stack=esl), qt=mk("c_qt", [128, TH], BF16, stack=esl),
                        kt=mk("c_kt", [128, TH], BF16, stack=esl),
                        AtA=mk("c_At", [128, HC, 128], BF16, stack=esl), ktmA=mk("c_ktm", [128, HC, 128], BF16, stack=esl),
                        kvA=mk("c_kv", [128, HC, 128], F32, stack=esl), SbfA=mk("c_Sbf", [128, HC, 128], BF16, nb=HC, stack=esl),
                        sc1=mk("c_sc", [128, 16], stack=esl), sc2=mk("c_sc2", [128, 24], stack=esl), sce=mk("c_sce", [128, 24], stack=esl)))
                def head_gen(h, fc, wq, wf, wg, Bs):
                    sg, gg, Bc, qsc, qt, kt = Bs["sg"], Bs["gg"], Bs["Bc"], Bs["qsc"], Bs["qt"], Bs["kt"]
                    qraw, sgate = Bs["qraw"], Bs["sgate"]
                    AtA, ktmA, kvA, SbfA, sc1, sc2, sce = Bs["AtA"], Bs["ktmA"], Bs["kvA"], Bs["SbfA"], Bs["sc1"], Bs["sc2"], Bs["sce"]
                    oT = gg
                    o2 = sg
                    lbc = dv.ap[:, 16 + h:17 + h]
                    omc = dv.ap[:, 32 + h:33 + h]
                    nomc = dv.ap[:, 48 + h:49 + h]
                    qscale = float(128 ** -0.5)

                    if has_s:
                        for hf_ in range(2):
                            bs_ = slice(hf_ * (NS // 2), (hf_ + 1) * (NS // 2))
                            C.dma("sp", hsb["S0hs"][h % 2][hf_].ap[:], D["sc"][bs_, h, :, :].rearrange("b k v -> k b v"), [],
                                  [hsb["S0hs"][h % 2][hf_].b])

                    def ev_f(pb, pbb, t0, tn):
                        act(sg.ap[:, t0:t0 + tn], pb[:, 0:tn], AF.Sigmoid, [pbb], [sg.b])
                    proj_fm(wf[0], wf[1], fc, xnT.ap, xnT.b, ncols, ev_f)
                    yield

                    def ev_q(pb, pbb, t0, tn):
                        if t0 < TH:
                            acopy(qraw.ap[:, t0:t0 + tn], pb[:, 0:tn], [pbb], [qraw.b])
                        else:
                            C.op("act", lambda: S_.mul(out=qsc.ap[:, 0:tn], in_=pb[:, 0:tn], mul=qscale), [pbb], [qsc.b])
                    proj_fm(wq[0], wq[1], fc, xnT.ap, xnT.b, ncols, ev_q)
                    yield

                    def ev_g(pb, pbb, t0, tn):
                        act(sgate.ap[:, t0:t0 + tn], pb[:, 0:tn], AF.Silu, [pbb], [sgate.b])
                    proj_fm(wg[0], wg[1], fc, xnT.ap, xnT.b, ncols, ev_g)
                    yield
                    act(gg.ap[:, 0:n], sg.ap[:, 0:n], AF.Ln, [sg.b, dv.b], [gg.b], scale=omc, bias=lbc)
                    ts(sg.ap[:, 0:n], sg.ap[:, 0:n], nomc, omc, ALU.mult, ALU.add, [sg.b, dv.b], [sg.b])
                    C.op("dve", lambda: V.tensor_tensor_scan(out=Bc.ap[:, 0:TH], data0=onesF.ap[:, 0:1].to_broadcast([128, TH]),
                                                             data1=gg.ap[:, 0:TH], initial=0.0, op0=ALU.mult, op1=ALU.add),
                         [cbuf, gg.b], [Bc.b])
                    yield
                    B3 = Bc.ap[:, 0:TH].rearrange("p (j t) -> p j t", t=128)
                    vcopy(sc1.ap[:, 0:8], B3[:, :, 63], [Bc.b], [sc1.b])
                    vcopy(sc1.ap[:, 8:16], B3[:, :, 127], [Bc.b], [sc1.b])
                    tt(gg.ap[:, 0:TH].rearrange("p (j t) -> p j t", t=128), B3, sc1.ap[:, 0:8].unsqueeze(2).to_broadcast([128, NCH, 128]),
                       ALU.subtract, [Bc.b, sc1.b], [gg.b])
                    vcopy(sc2.ap[:, 0:1], sc1.ap[:, 0:1], [sc1.b], [sc2.b])
                    tt(sc2.ap[:, 1:8], sc1.ap[:, 1:8], sc1.ap[:, 8:15], ALU.subtract, [sc1.b], [sc2.b])
                    vcopy(sc2.ap[:, 8:9], sc1.ap[:, 8:9], [sc1.b], [sc2.b])
                    tt(sc2.ap[:, 9:16], sc1.ap[:, 9:16], sc1.ap[:, 8:15], ALU.subtract, [sc1.b], [sc2.b])
                    tt(sc2.ap[:, 16:24], sc1.ap[:, 8:16], sc1.ap[:, 0:8], ALU.subtract, [sc1.b], [sc2.b])
                    yield
                    act(sce.ap[:], sc2.ap[:], AF.Exp, [sc2.b], [sce.b])
                    act(Bc.ap[:, 0:TH], gg.ap[:, 0:TH], AF.Exp, [gg.b], [Bc.b])
                    stt(qt.ap[:, 0:TH], qraw.ap[:, 0:TH], qscale, Bc.ap[:, 0:TH], ALU.mult, ALU.mult, [qraw.b, Bc.b], [qt.b])
                    yield
                    act(Bc.ap[:, 0:TH], gg.ap[:, 0:TH], AF.Exp, [gg.b, qt.b], [Bc.b], scale=-1.0)
                    tt(kt.ap[:, 0:TH], sg.ap[:, 0:TH], Bc.ap[:, 0:TH], ALU.mult, [sg.b, Bc.b], [kt.b])
                    yield
                    for hf in range(NCH // HC):
                        c0 = hf * HC * 128
                        pa, pab = bank()
                        for jj in range(HC):
                            cs = slice(c0 + jj * 128, c0 + (jj + 1) * 128)
                            mm(pa[:, jj * 128:(jj + 1) * 128], kt.ap[:, cs], qt.ap[:, cs], True, True, [kt.b, qt.b], [pab], True)
                        tt(AtA.ap[:], pa[:, 0:HC * 128].rearrange("p (j t) -> p j t", t=128), mle.ap[:].unsqueeze(1).to_broadcast([128, HC, 128]),
                           ALU.mult, [pab, mle.b], [AtA.b])
                        pk, pkb = bank()
                        pkv = pk[:, :].bitcast(BF16)
                        for jj in range(HC):
                            cs = slice(c0 + jj * 128, c0 + (jj + 1) * 128)
                            tp(pkv[:, jj * 128:(jj + 1) * 128], kt.ap[:, cs], identB.ap[:], [kt.b, identB.b], [pkb], True)
                        acopy(ktmA.ap[:], pkv[:, 0:HC * 128].rearrange("p (j t) -> p j t", t=128), [pkb], [ktmA.b])
                        yield
                        pS, pSb = bank()
                        for jj in range(HC):
                            j = hf * HC + jj
                            mm(pS[:, jj * 128:(jj + 1) * 128], ktmA.ap[:, jj, :], vtm.ap[:, j, fc:fc + 128], True, True, [ktmA.b, vtm.b], [pSb], True)
                        acopy(kvA.ap[:], pS[:, 0:HC * 128].rearrange("p (j t) -> p j t", t=128), [pSb], [kvA.b])
                        yield
                        for jj in range(HC):
                            j = hf * HC + jj
                            ts(SbfA.ap[:, jj, :], Sc.ap[:, h, :], sce.ap[:, j:j + 1], None, ALU.mult, None, [Sc.b[h], sce.b], [SbfA.b[jj]])
                            ts(Sc.ap[:, h, :], Sc.ap[:, h, :], sce.ap[:, 8 + j:9 + j], None, ALU.mult, None, [Sc.b[h], sce.b], [Sc.b[h]])
                            stt(Sc.ap[:, h, :], kvA.ap[:, jj, :], sce.ap[:, 16 + j:17 + j], Sc.ap[:, h, :], ALU.mult, ALU.add,
                                [kvA.b, sce.b, Sc.b[h]], [Sc.b[h]])
                        yield
                        po, pob = bank()
                        for jj in range(HC):
                            j = hf * HC + jj
                            cs = slice(c0 + jj * 128, c0 + (jj + 1) * 128)
                            mm(po[:, jj * 128:(jj + 1) * 128], vtm.ap[:, j, fc:fc + 128], AtA.ap[:, jj, :], True, False, [vtm.b, AtA.b], [pob], False)
                            mm(po[:, jj * 128:(jj + 1) * 128], SbfA.ap[:, jj, :], qt.ap[:, cs], False, True, [SbfA.b[jj], qt.b], [pob], True)
                        acopy(oT.ap[:, c0:c0 + HC * 128], po[:, 0:HC * 128], [pob], [oT.b])
                        yield
                    if has_s:
                        hgrn_samples(h, fc, sg, gg, qsc, vs, oT, hsb)
                        yield
                    act(qraw.ap[:, 0:n], oT.ap[:, 0:n], AF.Square, [oT.b], [qraw.b])
                    for (t0, tn) in tiles_of(ncols):
                        pb, pbb = bank()
                        mm(pb[:, 0:tn], onesB.ap[:], qraw.ap[:, t0:t0 + tn], True, True, [onesB.b, qraw.b], [pbb], True)
                        act(o2.ap[:, t0:t0 + tn], pb[:, 0:tn], AF.Ln, [pbb], [o2.b], scale=1.0 / 128.0, bias=EPS)
                    yield
                    act(o2.ap[:, 0:n], o2.ap[:, 0:n], AF.Exp, [o2.b], [o2.b], scale=-0.5)
                    stt(oT.ap[:, 0:n], oT.ap[:, 0:n], vcol("c_norm_w", h), o2.ap[:, 0:n], ALU.mult, ALU.mult, [oT.b, o2.b, vt.b], [oT.b])
                    tt(mixT.ap[:, h % 8, 0:n], oT.ap[:, 0:n], sgate.ap[:, 0:n], ALU.mult, [oT.b, sgate.b], [mixT.b[h % 8]])
                    yield

                for h4 in range(4):
                    wv = wload(Win, 8, 4096 + 512 * h4, 512)
                    wq = wload(Win, 8, 512 * h4, 512)
                    wf = wload(Win, 8, 2048 + 512 * h4, 512)
                    wg = wload(Win, 8, 6144 + 512 * h4, 512)
                    for j in range(NCH):
                        pb, pbb = bank()
                        for kc in range(8):
                            mm(pb[:, :], xnT.ap[:, kc, j * 128:(j + 1) * 128], wv[0][:, kc, :], kc == 0, kc == 7, [xnT.b[kc], wv[1]], [pbb], kc == 7)
                        acopy(vtm.ap[:, j, :], pb[:, :], [pbb], [vtm.b])
                    if has_s:
                        pb, pbb = bank()
                        for kc in range(8):
                            mm(pb[0:16, :], xnT.ap[:, kc, TH:TH + NS], wv[0][:, kc, :], kc == 0, kc == 7, [xnT.b[kc], wv[1]], [pbb], kc == 7)
                        vcopy(vs.ap[:, :], pb[0:16, :], [pbb], [vs.b])
                    fill = out_proj_gen(D["c_w_out"][0:1024, :], ncols) if h4 == 2 else None
                    run_streams([head_gen(4 * h4 + i, i * 128, wq, wf, wg, sets[i % nsets]) for i in range(4)], skew=SKEW_H, max_active=nsets,
                                filler=fill)
                C.barrier()

        def hgrn_alloc(esl):
            return dict(eg=mk("hs_eg", [128, NS], stack=esl), ktmS=mk("hs_ktm", [16, 128], BF16, stack=esl),
                        lm=mk("hs_lm", [128, 16, 128], BF16, stack=esl),
                        S0hs=[[mk("hs_S0a", [128, 8, 128], stack=esl), mk("hs_S0b", [128, 8, 128], stack=esl)] for _ in range(2)],
                        qb=mk("hs_qb", [128, NS], BF16, stack=esl),
                        od=mk("hs_od", [128, 16, 16], stack=esl), oh=mk("hs_oh", [16, 16], stack=esl))

        def hgrn_samples(h, fc, sg, gg, qsc, vs, oT, hsb):
            sl = slice(TH, TH + NS)
            eg, ktmS, lm, od, oh, qb = hsb["eg"], hsb["ktmS"], hsb["lm"], hsb["od"], hsb["oh"], hsb["qb"]
            vcopy(oh.ap[:], identF.ap[0:16, 0:16], [identF.b], [oh.b])
            act(eg.ap[:], gg.ap[:, sl], AF.Exp, [gg.b], [eg.b])
            pb, pbb = bank()
            tp(pb[0:16, 0:128], sg.ap[:, sl], identF.ap[:], [sg.b, identF.b], [pbb], True)
            vcopy(ktmS.ap[:], pb[0:16, 0:128], [pbb], [ktmS.b])
            HB = NS // 2
            S0h = hsb["S0hs"][h % 2]
            tt(lm.ap[0:16], ktmS.ap[:].unsqueeze(1).to_broadcast([16, 16, 128]), oh.ap[:].unsqueeze(2).to_broadcast([16, 16, 128]),
               ALU.mult, [ktmS.b, oh.b], [lm.b])
            pqs = [bank() for _ in range(4)]
            for b in range(NS):
                pq, pqb = pqs[b // 4]
                mm(pq[:, (b % 4) * 128:(b % 4 + 1) * 128], lm.ap[0:16, b, :], vs.ap[:, (h % 4) * 128:(h % 4 + 1) * 128], True, True, [lm.b, vs.b], [pqb], True)
            for hf in range(2):
                sh = S0h[hf]
                for bb in range(HB):
                    b = hf * HB + bb
                    act(sh.ap[:, bb, :], sh.ap[:, bb, :], AF.Copy, [sh.b, eg.b], [sh.b], scale=eg.ap[:, b:b + 1])
                for q4 in range(HB // 4):
                    pq, pqb = pqs[hf * (HB // 4) + q4]
                    tt(sh.ap[:, 4 * q4:4 * q4 + 4, :], sh.ap[:, 4 * q4:4 * q4 + 4, :], pq[:, :].rearrange("p (b v) -> p b v", v=128), ALU.add,
                       [sh.b, pqb], [sh.b])
                bs = slice(hf * HB, (hf + 1) * HB)
                C.dma("sp", O["cs"][bs, h, :, :].rearrange("b k v -> k b v"), sh.ap[:], [sh.b], [])
            for hf in range(2):
                (acopy if hf == 0 else vcopy)(lm.ap[:, hf * HB:(hf + 1) * HB, :], S0h[hf].ap[:], [S0h[hf].b], [lm.b])
            acopy(qb.ap[:], qsc.ap[:], [qsc.b], [qb.b])
            po, pob = bank()
            for b in range(NS):
                mm(po[:, b * 16:(b + 1) * 16], lm.ap[:, b, :], qb.ap[:], True, True, [lm.b, qb.b], [pob], b == NS - 1)
            tt(od.ap[:], po[:, 0:256].rearrange("p (b c) -> p b c", c=16), identBC.ap[:], ALU.mult, [pob, identBC.b], [od.b])
            C.op("dve", lambda: V.tensor_reduce(out=oT.ap[:, sl], in_=od.ap[:], axis=AX.X, op=ALU.add), [od.b], [oT.b])

        identBC = mk("identBC", [128, 16, 16])
        C.op("pool", lambda: G.memset(identBC.ap[:], 1.0), [], [identBC.b])
        C.op("pool", lambda: G.affine_select(out=identBC.ap[:], in_=identBC.ap[:], pattern=[[1, 16], [-1, 16]], compare_op=ALU.is_equal,
                                             fill=0.0, base=0, channel_multiplier=0), [identBC.b], [identBC.b])

        def store_out(ps_i, ncols, normed):
            with ExitStack() as esl:
                if normed:
                    rmsnorm(ncols, "norm_f", 0, hT)
                src = hT
                yo = [mk(f"yo{i}", [128, 1024], stack=esl) for i in range(2)]
                for j in range(TH // 128):
                    y = yo[j % 2]
                    for c0 in (0, 4):
                        pb, pbb = bank()
                        for c in range(4):
                            tp(pb[:, c * 128:(c + 1) * 128], src.ap[:, c0 + c, j * 128:(j + 1) * 128], identF.ap[:], [src.b[c0 + c], identF.b],
                               [pbb], c == 3)
                        if c0 == 0:
                            acopy(y.ap[:, 0:512], pb[:, :], [pbb], [y.b])
                        else:
                            vcopy(y.ap[:, 512:1024], pb[:, :], [pbb], [y.b])
                    r0 = ps_i * TH + j * 128
                    C.dma("sp", O["y_p"][r0:r0 + 128, :], y.ap[:], [y.b], [])
                if ncols > TH:
                    y = yo[0]
                    for c0 in (0, 4):
                        pb, pbb = bank()
                        for c in range(4):
                            tp(pb[0:16, c * 128:(c + 1) * 128], src.ap[:, c0 + c, TH:TH + NS], identF.ap[:], [src.b[c0 + c], identF.b], [pbb], c == 3)
                        vcopy(y.ap[0:16, c0 * 128:c0 * 128 + 512], pb[0:16, :], [pbb], [y.b])
                    C.dma("sp", O["y_s"][:, :], y.ap[0:16, :], [y.b], [])
                C.barrier()

        es_s = ExitStack()
        sacT = mk("sacT", [128, 8, 3, NS], stack=es_s)
        sbcT = mk("sbcT", [128, 12, 3, NS], stack=es_s)
        sahT = mk("sahT", [128, 8, NS], stack=es_s)
        xbcsT = mk("xbcsT", [128, 12, NS], stack=es_s)
        load_sample_states()
        for ps_i in range(2):
            ncols = NCOL if ps_i == 0 else TH
            phase0(ps_i, ncols)
            if DBG_STOP >= 1:
                if "N" in DBG_PARTS:
                    rmsnorm(ncols, "norm_w", 0, xnT)
                if "A" in DBG_PARTS:
                    layer0_A(ps_i, ncols)
                if "B" in DBG_PARTS:
                    layer0_B(ps_i, ncols)
                if ncols > TH and "S" in DBG_PARTS:
                    ssd_samples(0)
                    ssd_samples(1)
                if ps_i == 0:
                    C.barrier()
                    es_s.close()
                if "O" in DBG_PARTS:
                    ple(0, ps_i, ncols, pre=lambda: out_proj(D["ab_w_out"][1024:2048, :], ncols))
            if DBG_STOP >= 2:
                rmsnorm(ncols, "norm_w", 8, xnT)
                layer1(ps_i, ncols)
                ple(1, ps_i, ncols, pre=lambda: out_proj(D["c_w_out"][1024:2048, :], ncols))
            store_out(ps_i, ncols, DBG_STOP >= 2)
        if "D" in DBG_PARTS:
            C.dma("sp", O["y_p"][0:128, 0:64], dv.ap[:, :], [dv.b], [])
            C.dma("sp", O["y_p"][0:128, 64:320], vt.ap[:, :], [vt.b], [])
            C.dma("sp", O["y_p"][0:128, 320:328], hlast.ap[:, :], [hlast.b], [])
        hl3 = mk("hl3", [128, 8, 1])
        vcopy(hl3.ap[:, :, 0], hlast.ap[:, :], [hlast.b], [hl3.b])
        fm_to_rows(hl3, 8, 1, O["ahp"].rearrange("(o f) -> o f", o=1))
        fm_to_rows(tailA, 8, 3, O["acp"][:, :])
        fm_to_rows(tailB, 12, 3, O["bcp"][:, :])
        with ExitStack() as esl:
            so = mk("so", [128, 8, 128], stack=esl)
            for c0 in (0, 4):
                pb, pbb = bank()
                for c in range(4):
                    tp(pb[:, c * 128:(c + 1) * 128], Sssm.ap[:, (c0 + c) * 128:(c0 + c + 1) * 128], identF.ap[:], [Sssm_b[(c0 + c) // 4], identF.b],
                       [pbb], c == 3)
                vcopy(so.ap[:, c0:c0 + 4, :], pb[:, :].rearrange("p (c n) -> p c n", n=128), [pbb], [so.b])
            C.dma("sp", O["bsp"].rearrange("(c h2) q n -> (h2 q) c n", h2=2), so.ap[:], [so.b], [])
            C.dma("sp", O["cp"].rearrange("h k v -> k h v"), Sc.ap[:], Sc.b, [])
            fm_to_rows(ahsT, 8, NS, O["ahs"][:, :])
            fm_to_rows(axsT, 8, NS, O["acs"][:, 2, :])
            fm_to_rows(bxsT, 12, NS, O["bcs"][:, 2, :])
        C.barrier(engines=("sp",), with_pool=True)
        _NC_CACHE["marks"] = C.marks
        _NC_CACHE["nwait"] = C.nwait
    return nc


IN_SHAPES = {
    "xp": [2048, 1024], "xs": [16, 1024], "pp": [2, 2048, 256], "psm": [2, 16, 256],
    "sah": [16, 1024], "sac": [16, 3, 1024], "sbs": [16, 16, 64, 128], "sbc": [16, 3, 1536], "sc": [16, 16, 128, 128],
    "norm_w": [2, 1024], "norm_f": [1024], "ab_w_in": [1024, 4624], "a_conv_w": [4, 1024], "a_conv_b": [1024],
    "a_w_r": [8, 128, 128], "a_b_r": [1024], "a_w_i": [8, 128, 128], "a_b_i": [1024], "a_lam": [1024],
    "b_conv_w": [4, 1536], "b_conv_b": [1536], "b_dt_bias": [16], "b_a_log": [16], "b_d": [16], "b_norm_w": [1024],
    "ab_w_out": [2048, 1024], "c_w_in": [1024, 8192], "c_lb": [2, 2048], "c_norm_w": [2048], "c_w_out": [2048, 1024],
    "ple_proj": [2, 256, 1024], "ple_gate": [2, 1024, 1024],
}
OUT_SHAPES = {
    "y_p": [2048, 1024], "y_s": [16, 1024], "ahp": [1024], "acp": [3, 1024], "bsp": [16, 64, 128], "bcp": [3, 1536],
    "cp": [16, 128, 128], "ahs": [16, 1024], "acs": [16, 3, 1024], "bss": [16, 16, 64, 128], "bcs": [16, 3, 1536],
    "cs": [16, 16, 128, 128],
}

_NC_CACHE = {}


def _in_maps(inputs, n=8):
    f = lambda a: np.ascontiguousarray(np.asarray(a, dtype=np.float32))
    I = {k: np.asarray(v) for k, v in inputs.items()}
    shared = {
        "norm_w": I["norm_w"], "norm_f": I["norm_f"], "ab_w_in": I["ab_w_in"][0], "a_conv_w": I["a_conv_w"][0],
        "a_conv_b": I["a_conv_b"][0], "a_w_r": I["a_w_r"][0], "a_b_r": I["a_b_r"][0], "a_w_i": I["a_w_i"][0],
        "a_b_i": I["a_b_i"][0], "a_lam": I["a_lam"][0], "b_conv_w": I["b_conv_w"][0], "b_conv_b": I["b_conv_b"][0],
        "b_dt_bias": I["b_dt_bias"][0], "b_a_log": I["b_a_log"][0], "b_d": I["b_d"][0], "b_norm_w": I["b_norm_w"][0],
        "ab_w_out": I["ab_w_out"][0], "c_w_in": I["c_w_in"][0], "c_lb": I["c_lb"], "c_norm_w": I["c_norm_w"][0],
        "c_w_out": I["c_w_out"][0], "ple_proj": I["ple_proj"], "ple_gate": I["ple_gate"],
    }
    shared = {k: f(v) for k, v in shared.items()}
    maps = []
    for c in range(n):
        s = slice(16 * c, 16 * c + 16)
        m = dict(shared)
        m["xp"] = f(I["x_prompt"][c])
        m["xs"] = f(I["x_sample"][s, 0])
        m["pp"] = f(I["p_prompt"][:, c])
        m["psm"] = f(I["p_sample"][:, s, 0])
        m["sah"] = f(I["state_a_h"][0, s])
        m["sac"] = f(I["state_a_conv"][0, s])
        m["sbs"] = f(I["state_b_ssm"][0, s])
        m["sbc"] = f(I["state_b_conv"][0, s])
        m["sc"] = f(I["state_c"][0, s])
        maps.append(m)
    return maps


def _assemble(results):
    cat = lambda k, ax=0: np.concatenate([np.asarray(r[k], dtype=np.float32) for r in results], axis=ax)
    stk = lambda k: np.stack([np.asarray(r[k], dtype=np.float32) for r in results], axis=0)
    y_p = stk("y_p")
    y_s = cat("y_s")[:, None, :]
    return (y_p, y_s,
            stk("ahp")[None], stk("acp")[None], stk("bsp")[None], stk("bcp")[None], stk("cp")[None],
            cat("ahs")[None], cat("acs")[None], cat("bss")[None], cat("bcs")[None], cat("cs")[None])


def kernel(**inputs):
    if "nc" not in _NC_CACHE:
        _NC_CACHE["nc"] = build_program()
    nc = _NC_CACHE["nc"]
    maps = _in_maps(inputs)
    res = run_bass_kernel_spmd(nc, maps, core_ids=list(range(8)))
    return _assemble(res.results)
```
